# Optimizing a Trainium2 kernel written in Bass

```python
import jax, jax.numpy as jnp
from jax import lax
import numpy as np

D_MODEL = 2048
BATCH = 16
SEQ = 256
DEPTH = 2
DEC_BATCH = 2
DEC_SEQ = 1024
PAST_LEN = 512

GRID_W = 64
CHUNK_MLP = 128
A_WIDTH = D_MODEL // 4
A_GROUPS = 4
A_GDIM = A_WIDTH // A_GROUPS
B_HDIM = 64
B_WIDTH = 3 * D_MODEL // 8
B_HEADS = B_WIDTH // B_HDIM
DECAY_LORA = 64
AAA_LORA = 64
GATE_LORA = 128
C_KDIM = 128
C_WIDTH = D_MODEL - A_WIDTH - B_WIDTH
C_HEADS = C_WIDTH // C_KDIM
C_VDIM = C_WIDTH // C_HEADS
C_CHUNK = 16
D_FF = 5632
A_COLS = 2 * A_WIDTH
B_COLS = 3 * B_WIDTH + 2 * DECAY_LORA + 2 * AAA_LORA + GATE_LORA
C_COLS = 5 * C_WIDTH
IN_COLS = A_COLS + B_COLS + C_COLS
MIX_WIDTH = A_WIDTH + B_WIDTH + C_WIDTH
B_SPLITS = (B_WIDTH, 2 * B_WIDTH, 3 * B_WIDTH, 3 * B_WIDTH + 2 * DECAY_LORA,
            3 * B_WIDTH + 2 * DECAY_LORA + 2 * AAA_LORA)
N_MOD = 9
EPS = 1e-6
LNX_EPS = 64e-5

kernel_name = 'hybrid_gmlp_rwkv7_hgrn2_diffusion_step'


def rmsnorm(x, g):
    xf = x.astype(jnp.float32)
    y = xf * lax.rsqrt(jnp.mean(xf * xf, axis=-1, keepdims=True) + EPS)
    return (y * g.astype(jnp.float32)).astype(x.dtype)


def swiglu(h, w1, w3, w2):
    return (jax.nn.silu(h @ w1) * (h @ w3)) @ w2


def centred_shift(x):
    prev = jnp.pad(x[:, :-1], ((0, 0), (1, 0), (0, 0)))
    nxt = jnp.pad(x[:, 1:], ((0, 0), (0, 1), (0, 0)))
    return 0.5 * (prev + nxt)


def chunk_mlp_mix(za, n_chunks, ng, ws, bs):
    bsz, t = za.shape[0], za.shape[1]
    u, v = jnp.split(jax.nn.gelu(za), 2, axis=-1)
    v = rmsnorm(v.reshape(bsz, n_chunks, CHUNK_MLP, A_GROUPS, A_GDIM), ng)
    v = jnp.einsum('gpq,bnqgc->bnpgc', ws, v) + bs.T[:, :, None]
    return u * v.reshape(bsz, t, A_WIDTH)


def rwkv_scan(r, w, k, v, a, b, s0):
    def step(S, inp):
        r_t, w_t, k_t, v_t, a_t, b_t = inp
        sa = jnp.einsum('bhvk,bhk->bhv', S, a_t)
        S = S * w_t[:, :, None, :] + sa[..., None] * b_t[:, :, None, :] + v_t[..., None] * k_t[:, :, None, :]
        return S, jnp.einsum('bhvk,bhk->bhv', S, r_t)
    xs = tuple(jnp.moveaxis(z, 1, 0) for z in (r, w, k, v, a, b))
    s_fin, y = lax.scan(step, s0.astype(jnp.float32), xs)
    return jnp.moveaxis(y, 0, 1), s_fin


def rwkv7_mix(zb, s0, mu, w0, w2, a0, a2, g2, kk_w, ka_w, r_k, lnx_g, lnx_b):
    zb = zb + (centred_shift(zb) - zb) * mu
    bsz, t = zb.shape[0], zb.shape[1]
    r, k, v, wd, ad, gd = jnp.split(zb, B_SPLITS, axis=-1)
    wd = wd.reshape(bsz, t, 2, DECAY_LORA)
    ad = ad.reshape(bsz, t, 2, AAA_LORA)
    w_raw = w0 + jnp.einsum('btdr,drc->btdc', jnp.tanh(wd), w2)
    decay = jnp.exp(-jnp.exp(-jax.nn.softplus(-w_raw) - 0.5))
    a = jax.nn.sigmoid(a0 + jnp.einsum('btdr,drc->btdc', ad, a2))
    gate = jax.nn.sigmoid(gd) @ g2
    kd = k[:, :, None] * (1.0 + (a - 1.0) * ka_w)
    hd = lambda z: z.reshape(z.shape[:-1] + (B_HEADS, B_HDIM))
    kk = hd(k * kk_w)
    kk = kk / jnp.maximum(jnp.sqrt(jnp.sum(kk * kk, axis=-1, keepdims=True)), 1e-12)
    r_h, v_h = hd(r), hd(v)
    ys, states, bonus = [], [], []
    for d in range(2):
        k_d = hd(kd[:, :, d])
        seq = (r_h, hd(decay[:, :, d]), k_d, v_h, -kk, kk * hd(a[:, :, d]))
        if d == 1:
            seq = tuple(jnp.flip(s_, axis=1) for s_ in seq)
        y, s_fin = rwkv_scan(*seq, s0[:, d])
        ys.append(jnp.flip(y, axis=1) if d == 1 else y)
        states.append(s_fin)
        bonus.append(jnp.sum(r_h * k_d * r_k, axis=-1, keepdims=True) * v_h)
    y = ys[0] + ys[1]
    mean = jnp.mean(y, axis=-1, keepdims=True)
    var = jnp.mean(jnp.square(y - mean), axis=-1, keepdims=True)
    y = (y - mean) * lax.rsqrt(var + LNX_EPS) * lnx_g + lnx_b + bonus[0] + bonus[1]
    return y.reshape(bsz, t, B_WIDTH) * gate, jnp.stack(states, axis=1)


def gla_chunked(q, k, v, logf, s0):
    bsz, t, nh, dk = q.shape
    dv = v.shape[-1]
    nc = t // C_CHUNK
    q, k, logf = (z.reshape(bsz, nc, C_CHUNK, nh, dk) for z in (q, k, logf))
    v = v.reshape(bsz, nc, C_CHUNK, nh, dv)
    b = jnp.cumsum(logf, axis=2)
    mask = jnp.tril(jnp.ones((C_CHUNK, C_CHUNK), dtype=bool))[None, None, :, :, None, None]
    dec = jnp.exp(jnp.where(mask, b[:, :, :, None] - b[:, :, None, :], -jnp.inf))
    att = jnp.einsum('bnihk,bnjhk,bnijhk->bnhij', q, k, dec)
    o_intra = jnp.einsum('bnhij,bnjhv->bnihv', att, v)
    b_end = b[:, :, -1]
    kv = jnp.einsum('bnjhk,bnjhv->bnhkv', k * jnp.exp(b_end[:, :, None] - b), v)

    def step(S, inp):
        cdec, kv_c = inp
        return cdec[..., None] * S + kv_c, S
    s_fin, s_start = lax.scan(step, s0.astype(jnp.float32),
                              (jnp.moveaxis(jnp.exp(b_end), 1, 0), jnp.moveaxis(kv, 1, 0)))
    s_start = jnp.moveaxis(s_start, 0, 1)
    o_inter = jnp.einsum('bnihk,bnhkv->bnihv', q * jnp.exp(b), s_start)
    return (o_intra + o_inter).reshape(bsz, t, nh, dv), s_fin


def hgrn2_mix(zc, s0, lb, gn_g):
    bsz, t = zc.shape[0], zc.shape[1]
    q, f2, iv, g = jnp.split(zc, (C_WIDTH, 3 * C_WIDTH, 4 * C_WIDTH), axis=-1)
    hk = lambda z: z.reshape(bsz, t, C_HEADS, C_KDIM)
    hv = lambda z: z.reshape(bsz, t, C_HEADS, C_VDIM)
    q = hk(jax.nn.silu(q))
    v = hv(iv)
    fg = lb + (1.0 - lb) * jax.nn.sigmoid(f2.reshape(bsz, t, 2, C_WIDTH))
    logf = jnp.log(fg)
    kg = 1.0 - fg
    outs, states = [], []
    for d in range(2):
        seq = (q, hk(kg[:, :, d]), v, hk(logf[:, :, d]))
        if d == 1:
            seq = tuple(jnp.flip(s_, axis=1) for s_ in seq)
        o, s_fin = gla_chunked(*seq, s0[:, d])
        outs.append(jnp.flip(o, axis=1) if d == 1 else o)
        states.append(s_fin)
    o = rmsnorm(outs[0] + outs[1], gn_g) * jax.nn.silu(hv(g))
    return o.reshape(bsz, t, C_WIDTH), jnp.stack(states, axis=1)


def trunk_layer(x, cond, s_rw0, s_hg0, n_chunks, l, P):
    mod = jax.nn.silu(cond) @ P['w_mod'][l] + P['b_mod'][l]
    mod = mod.reshape(cond.shape[0], 1, N_MOD, D_MODEL).astype(x.dtype)
    sh1, sc1, gt1, sh2, sc2, gt2, sh3, sc3, gt3 = [mod[:, :, i] for i in range(N_MOD)]
    h = rmsnorm(x, P['norm_g'][l, 0]) * (1 + sc1) + sh1
    x = x + 0.5 * gt1 * swiglu(h, P['ffn_w1'][l, 0], P['ffn_w3'][l, 0], P['ffn_w2'][l, 0])
    h = rmsnorm(x, P['norm_g'][l, 1]) * (1 + sc2) + sh2
    z = (h @ P['w_in'][l]).astype(jnp.float32)
    za, zb, zc = jnp.split(z, (A_COLS, A_COLS + B_COLS), axis=-1)
    oa = chunk_mlp_mix(za, n_chunks, P['mlp_norm_g'][l], P['mlp_ws'][l], P['mlp_bs'][l])
    ob, s_rw = rwkv7_mix(zb, s_rw0, P['rwkv_mu'][l], P['rwkv_w0'][l], P['rwkv_w2'][l],
                         P['rwkv_a0'][l], P['rwkv_a2'][l], P['rwkv_g2'][l], P['rwkv_kk'][l],
                         P['rwkv_ka'][l], P['rwkv_rk'][l], P['rwkv_lnx_g'][l], P['rwkv_lnx_b'][l])
    oc, s_hg = hgrn2_mix(zc, s_hg0, P['hgrn_lb'][l], P['hgrn_gn'][l])
    o = jnp.concatenate([oa, ob, oc], axis=-1).astype(x.dtype)
    x = x + gt2 * (o @ P['w_out'][l])
    h = rmsnorm(x, P['norm_g'][l, 2]) * (1 + sc3) + sh3
    x = x + 0.5 * gt3 * swiglu(h, P['ffn_w1'][l, 1], P['ffn_w3'][l, 1], P['ffn_w2'][l, 1])
    return x, s_rw, s_hg


def setup_inputs(seed: int = 0) -> dict:
    key = jax.random.key(seed)
    ks = iter(jax.random.split(key, 40))
    f32 = jnp.float32
    nrm = lambda shape, s: jax.random.normal(next(ks), shape, f32) * s
    uni = lambda shape, lo, hi: jax.random.uniform(next(ks), shape, f32, lo, hi)
    return {
        'x_prompt': nrm((BATCH, SEQ, D_MODEL), 1.0),
        'x_sample': nrm((DEC_BATCH, DEC_SEQ, D_MODEL), 1.0),
        'state_rwkv': nrm((DEC_BATCH, DEPTH, 2, B_HEADS, B_HDIM, B_HDIM), 0.5),
        'state_hgrn': nrm((DEC_BATCH, DEPTH, 2, C_HEADS, C_KDIM, C_VDIM), 0.5),
        'c': nrm((DEC_BATCH, D_MODEL), 1.0),
        'c_ctx': nrm((D_MODEL,), 1.0),
        'norm_g': 1.0 + nrm((DEPTH, 3, D_MODEL), 0.05),
        'w_mod': nrm((DEPTH, D_MODEL, N_MOD * D_MODEL), 0.5 * D_MODEL ** -0.5),
        'b_mod': nrm((DEPTH, N_MOD * D_MODEL), 0.02),
        'ffn_w1': nrm((DEPTH, 2, D_MODEL, D_FF), D_MODEL ** -0.5),
        'ffn_w3': nrm((DEPTH, 2, D_MODEL, D_FF), D_MODEL ** -0.5),
        'ffn_w2': nrm((DEPTH, 2, D_FF, D_MODEL), D_FF ** -0.5),
        'w_in': nrm((DEPTH, D_MODEL, IN_COLS), D_MODEL ** -0.5),
        'w_out': nrm((DEPTH, MIX_WIDTH, D_MODEL), MIX_WIDTH ** -0.5),
        'mlp_norm_g': 1.0 + nrm((DEPTH, A_GROUPS, A_GDIM), 0.05),
        'mlp_ws': nrm((DEPTH, A_GROUPS, CHUNK_MLP, CHUNK_MLP), CHUNK_MLP ** -0.5),
        'mlp_bs': 1.0 + nrm((DEPTH, A_GROUPS, CHUNK_MLP), 0.1),
        'rwkv_mu': uni((DEPTH, B_COLS), 0.0, 1.0),
        'rwkv_w0': uni((DEPTH, 2, B_WIDTH), -6.0, -1.0),
        'rwkv_w2': nrm((DEPTH, 2, DECAY_LORA, B_WIDTH), 0.1 * DECAY_LORA ** -0.5),
        'rwkv_a0': nrm((DEPTH, 2, B_WIDTH), 0.1),
        'rwkv_a2': nrm((DEPTH, 2, AAA_LORA, B_WIDTH), 0.5 * AAA_LORA ** -0.5),
        'rwkv_g2': nrm((DEPTH, GATE_LORA, B_WIDTH), GATE_LORA ** -0.5),
        'rwkv_kk': 0.85 + nrm((DEPTH, B_WIDTH), 0.05),
        'rwkv_ka': 1.0 + nrm((DEPTH, B_WIDTH), 0.05),
        'rwkv_rk': nrm((DEPTH, B_HEADS, B_HDIM), 0.1),
        'rwkv_lnx_g': 1.0 + nrm((DEPTH, B_HEADS, B_HDIM), 0.05),
        'rwkv_lnx_b': nrm((DEPTH, B_HEADS, B_HDIM), 0.02),
        'hgrn_lb': nrm((DEPTH, 2, C_WIDTH), 1.0),
        'hgrn_gn': 1.0 + nrm((DEPTH, C_VDIM), 0.05),
        'final_g': 1.0 + nrm((D_MODEL,), 0.05),
    }


def reference(x_prompt, x_sample, state_rwkv, state_hgrn, c, c_ctx, norm_g, w_mod, b_mod,
              ffn_w1, ffn_w3, ffn_w2, w_in, w_out, mlp_norm_g, mlp_ws, mlp_bs, rwkv_mu,
              rwkv_w0, rwkv_w2, rwkv_a0, rwkv_a2, rwkv_g2, rwkv_kk, rwkv_ka, rwkv_rk,
              rwkv_lnx_g, rwkv_lnx_b, hgrn_lb, hgrn_gn, final_g):
    sm = jax.nn.softmax(hgrn_lb.astype(jnp.float32), axis=0)
    lower_bounds = jnp.cumsum(sm, axis=0) - sm[0]
    P = {'w_mod': w_mod, 'b_mod': b_mod, 'norm_g': norm_g, 'ffn_w1': ffn_w1, 'ffn_w3': ffn_w3,
         'ffn_w2': ffn_w2, 'w_in': w_in, 'w_out': w_out, 'mlp_norm_g': mlp_norm_g,
         'mlp_ws': mlp_ws, 'mlp_bs': mlp_bs, 'rwkv_mu': rwkv_mu, 'rwkv_w0': rwkv_w0,
         'rwkv_w2': rwkv_w2, 'rwkv_a0': rwkv_a0, 'rwkv_a2': rwkv_a2, 'rwkv_g2': rwkv_g2,
         'rwkv_kk': rwkv_kk, 'rwkv_ka': rwkv_ka, 'rwkv_rk': rwkv_rk, 'rwkv_lnx_g': rwkv_lnx_g,
         'rwkv_lnx_b': rwkv_lnx_b, 'hgrn_lb': lower_bounds, 'hgrn_gn': hgrn_gn}

    bsz, ctx_len = x_prompt.shape[0], x_prompt.shape[1]
    x = x_prompt
    rw_states, hg_states = [], []
    for l in range(DEPTH):
        s_rw0 = jnp.zeros((bsz, 2, B_HEADS, B_HDIM, B_HDIM), jnp.float32)
        s_hg0 = jnp.zeros((bsz, 2, C_HEADS, C_KDIM, C_VDIM), jnp.float32)
        x, s_rw, s_hg = trunk_layer(x, c_ctx[None], s_rw0, s_hg0, ctx_len // CHUNK_MLP, l, P)
        rw_states.append(s_rw)
        hg_states.append(s_hg)
    y_prompt = rmsnorm(x, final_g)
    new_state_rwkv = jnp.stack(rw_states, axis=1)
    new_state_hgrn = jnp.stack(hg_states, axis=1)

    rows = x_sample.shape[1] // GRID_W
    n_chunks = rows * GRID_W // CHUNK_MLP
    x = x_sample
    for l in range(DEPTH):
        x, _, _ = trunk_layer(x, c, state_rwkv[:, l], state_hgrn[:, l], n_chunks, l, P)
    y_sample = rmsnorm(x, final_g)
    return (y_prompt, y_sample, new_state_rwkv, new_state_hgrn)
```

```python
import numpy as np
from contextlib import ExitStack
import concourse.bass as bass
import concourse.mybir as mybir
from concourse.bass_utils import run_bass_kernel_spmd

F32 = mybir.dt.float32
BF16 = mybir.dt.bfloat16
AF = mybir.ActivationFunctionType
ALU = mybir.AluOpType

EPOCH = 16000
NDMA = 24

D = 2048
KC = 16
T = 1024
NT = 8
DFF = 5632
NFC = 44
DEPTH = 2
IN_COLS = 7552
EPS = 1e-6
LNX_EPS = 64e-5
NPV = 324
ARENA_B = 96 * 1024


def _dsize(dt):
    return 2 if dt == BF16 else 4


class Prog:
    ENGS = ('pe', 'act', 'dve', 'pool', 'sp')

    def __init__(self, nc):
        self.nc = nc
        self.streams = {e: [] for e in self.ENGS}
        self.cnt = {e: 0 for e in ('pe', 'act', 'dve', 'pool')}
        self.seen = {e: {} for e in self.ENGS}
        self.recs = {}
        self.dma_cnt = [0] * NDMA
        self.dma_rr = 0
        self.semkeys = set()

    def _region(self, ap):
        t = ap.tensor
        name = t.name
        dims = ap.ap
        off = ap.offset
        sp = str(ap.space)
        es = _dsize(ap.dtype)
        if 'PSUM' in sp:
            return name, (0, 128, 0, 2048)
        if 'SB' in sp:
            row = 1
            for s in t.shape[1:]:
                row *= s
            p0 = off // row
            f0 = off % row
            pc = dims[0][1]
            lo = f0
            hi = f0
            for st, c in dims[1:]:
                if st < 0:
                    lo += st * (c - 1)
                else:
                    hi += st * (c - 1)
            return name, (p0, p0 + pc, lo * es, (hi + 1) * es)
        lo = off
        hi = off
        for st, c in dims:
            if st < 0:
                lo += st * (c - 1)
            else:
                hi += st * (c - 1)
        return name, (0, 1, lo * es, (hi + 1) * es)

    @staticmethod
    def _ov(a, b):
        return a[0] < b[1] and b[0] < a[1] and a[2] < b[3] and b[2] < a[3]

    @staticmethod
    def _contains(a, b):
        return a[0] <= b[0] and b[1] <= a[1] and a[2] <= b[2] and b[3] <= a[3]

    def _deps(self, eng, reads, writes):
        waits = {}
        acc = [(self._region(a), False) for a in reads] + [(self._region(a), True) for a in writes]
        for (name, rg), isw in acc:
            for (r_rg, r_w, r_key, r_val, r_eng) in self.recs.get(name, ()):
                if not (r_w or isw):
                    continue
                if not self._ov(rg, r_rg):
                    continue
                if eng == 'pe' and r_eng == 'pe':
                    continue
                if self.seen[eng].get(r_key, 0) >= r_val:
                    continue
                if waits.get(r_key, 0) < r_val:
                    waits[r_key] = r_val
        for k, v in waits.items():
            self.seen[eng][k] = v
        return acc, list(waits.items())

    def _record(self, acc, key, val, eng):
        for (name, rg), isw in acc:
            lst = self.recs.setdefault(name, [])
            if isw:
                lst[:] = [r for r in lst if not self._contains(rg, r[0])]
            elif key[0] != 'dma':
                lst[:] = [r for r in lst if not (r[2][0] == key[0] and (not r[1]) and r[4] == eng
                                                 and self._contains(rg, r[0]))]
            lst.append((rg, isw, key, val, eng))

    def op(self, eng, fn, reads=(), writes=()):
        acc, waits = self._deps(eng, reads, writes)
        n = self.cnt[eng]
        self.cnt[eng] = n + 1
        key = (eng, n // EPOCH)
        val = n % EPOCH + 1
        self.semkeys.add(key)
        self.streams[eng].append((fn, waits, (key, 1)))
        self._record(acc, key, val, eng)

    def dma(self, eng, out, in_, **kw):
        j = self.dma_rr
        self.dma_rr = (j + 1) % NDMA
        key = ('dma', j)
        self.semkeys.add(key)
        acc, waits = self._deps(eng, [in_], [out])
        prev = self.dma_cnt[j] * 16
        wd = dict(waits)
        if prev > 0 and self.seen[eng].get(key, 0) < prev:
            wd[key] = max(wd.get(key, 0), prev)
            self.seen[eng][key] = prev
        self.dma_cnt[j] += 1
        val = self.dma_cnt[j] * 16
        self.streams[eng].append((lambda e: e.dma_start(out=out, in_=in_, **kw), list(wd.items()), (key, 16)))
        self._record(acc, key, val, eng)

    def final_wait_all(self, eng='sp'):
        waits = []
        for j in range(NDMA):
            if self.dma_cnt[j]:
                waits.append((('dma', j), self.dma_cnt[j] * 16))
        for e, n in self.cnt.items():
            if n:
                waits.append(((e, (n - 1) // EPOCH), (n - 1) % EPOCH + 1))
        self.streams[eng].append((None, waits, None))

    def build(self):
        nc = self.nc
        keys = sorted(self.semkeys, key=str)
        with ExitStack() as es:
            sems = {k: es.enter_context(nc.semaphore("s_%s_%d" % (k[0], k[1]))) for k in keys}
            block = es.enter_context(nc.Block())

            def emit(stream):
                def body(e):
                    for fn, waits, inc in stream:
                        for k, v in waits:
                            e.wait_ge(sems[k], v)
                        if fn is None:
                            continue
                        ins = fn(e)
                        if inc is not None:
                            ins.then_inc(sems[inc[0]], inc[1])
                return body

            if self.streams['sp']:
                block.sync(emit(self.streams['sp']))
            if self.streams['pe']:
                block.tensor(emit(self.streams['pe']))
            if self.streams['act']:
                block.scalar(emit(self.streams['act']))
            if self.streams['dve']:
                block.vector(emit(self.streams['dve']))
            if self.streams['pool']:
                block.gpsimd(emit(self.streams['pool']))


R_NORMG = 0
R_BMOD = 48
R_MU = 192
R_W0 = 213
R_A0 = 225
R_KK = 237
R_KA = 243
R_RK = 249
R_LNG = 255
R_LNB = 261
R_LB = 267
R_GN = 291
R_FG = 292
R_COND = 308

C_MT2F = 0
C_MT2B = 256
C_SL = 512
C_SU = 640
C_HMF = 768
C_HMB = 896
C_BD64 = 1024
C_BM = 1152
C_RM = 1160
NCONST = 1288

M_SLAB = 0
M_OCH = 24576
M_WOUT = 32768
M_SCR = 49152
DECAY_C = 0.6065306597126334

ALL_PARTS = ('mod', 'ffn1', 'a', 'b', 'c', 'ffn2')


def build_program(depth=DEPTH, parts=ALL_PARTS, dbg=False, rwk=(6, 2, 8, 9)):
    nc = bass.Bass("TRN2", target_bir_lowering=False)
    dram_in = lambda name, shape: nc.dram_tensor(name, list(shape), F32, kind="ExternalInput").ap()
    dram_out = lambda name, shape: nc.dram_tensor(name, list(shape), F32, kind="ExternalOutput").ap()
    x_in = dram_in("x_in", [T, D])
    pvec = dram_in("pvec", [depth * NPV, 128])
    consts_d = dram_in("consts", [128, NCONST])
    flags_d = dram_in("flags", [128, 2])
    srw0 = dram_in("srw0", [depth * 2 * 12 * 64, 64])
    shg0 = dram_in("shg0", [depth * 2 * 6 * 128, 128])
    w_mod = dram_in("w_mod", [depth * D, 9 * D])
    ffn_w1 = dram_in("ffn_w1", [depth * 2 * D, DFF])
    ffn_w3 = dram_in("ffn_w3", [depth * 2 * D, DFF])
    ffn_w2 = dram_in("ffn_w2", [depth * 2 * DFF, D])
    w_inp = dram_in("w_inp", [depth * D, IN_COLS])
    w_out = dram_in("w_out", [depth * D, D])
    wsT_d = dram_in("wsT", [depth * 4 * 128, 128])
    ngb_d = dram_in("ngb", [depth, 512])
    bsb_d = dram_in("bsb", [depth, 512])
    rw2_d = dram_in("rw2", [depth * 128, 768])
    ra2_d = dram_in("ra2", [depth * 128, 768])
    rg2_d = dram_in("rg2", [depth * 128, 768])
    y_out = dram_out("y_out", [T, D])
    srw_out = dram_out("srw_out", [4 * depth * 2 * 12 * 64, 64])
    shg_out = dram_out("shg_out", [4 * depth * 2 * 6 * 128, 128])
    dbg_out = dram_out("dbg_out", [128, 16 * T]) if dbg else None

    P = Prog(nc)
    with ExitStack() as es:
        sb = lambda name, shape, dt=F32: es.enter_context(nc.sbuf_tensor(name, list(shape), dt))
        xT = sb("xT", [128, KC, T])
        hT = sb("hT", [128, KC, T], BF16)
        arena = sb("arena", [128, ARENA_B // 4])
        pvT = sb("pvT", [128, NPV])
        modT = sb("modT", [128, 144])
        dv = sb("dv", [128, 6, KC])
        dv2 = sb("dv2", [128, 120])
        scb = sb("scb", [128, KC], BF16)
        ident = sb("ident", [128, 128])
        ones = sb("ones", [128, 128])
        cst = sb("cst", [128, NCONST])
        flg = sb("flg", [128, 2])
        psb = [es.enter_context(nc.psum_tensor("ps%d" % i, [128, 512], F32)) for i in range(8)]
        ar32 = arena[:]
        ar16 = arena[:].bitcast(BF16)

        def a32(off_b, *shape):
            n = int(np.prod(shape))
            assert off_b % 4 == 0 and off_b + 4 * n <= ARENA_B, (off_b, shape)
            ap = ar32[:, off_b // 4: off_b // 4 + n]
            if len(shape) == 2:
                ap = ap.rearrange("p (a b) -> p a b", a=shape[0])
            return ap

        def a16(off_b, *shape):
            n = int(np.prod(shape))
            assert off_b % 4 == 0 and off_b + 2 * n <= ARENA_B, (off_b, shape)
            ap = ar16[:, off_b // 2: off_b // 2 + n]
            if len(shape) == 2:
                ap = ap.rearrange("p (a b) -> p a b", a=shape[0])
            return ap

        def rev(ap):
            n = ap.ap[-1][1]
            assert len(ap.ap) == 2 and ap.ap[-1][0] == 1
            return bass.AP(ap.tensor, ap.offset + n - 1, [list(ap.ap[0]), [-1, n]])

        def mm(out, lhsT, rhs, start=True, stop=True):
            P.op('pe', lambda e: e.matmul(out, lhsT=lhsT, rhs=rhs, start=start, stop=stop), [lhsT, rhs], [out])

        def tr(out, in_, idn):
            P.op('pe', lambda e: e.transpose(out=out, in_=in_, identity=idn), [in_, idn], [out])

        def act(out, in_, func, bias=None, scale=None, accum=None):
            kw = {}
            rd = [in_]
            wr = [out]
            if bias is not None:
                kw['bias'] = bias
                if not isinstance(bias, (int, float)):
                    rd.append(bias)
            if scale is not None:
                kw['scale'] = scale
                if not isinstance(scale, (int, float)):
                    rd.append(scale)
            if accum is not None:
                kw['accum_out'] = accum
                wr.append(accum)
            P.op('act', lambda e: e.activation(out=out, in_=in_, func=func, **kw), rd, wr)

        def tt(eng, out, in0, in1, op):
            P.op(eng, lambda e: e.tensor_tensor(out=out, in0=in0, in1=in1, op=op), [in0, in1], [out])

        def ts(eng, out, in0, s1, op0, s2=None, op1=None):
            rd = [in0] + [s for s in (s1, s2) if s is not None and not isinstance(s, (int, float))]
            if op1 is None:
                P.op(eng, lambda e: e.tensor_scalar(out=out, in0=in0, scalar1=s1, scalar2=None, op0=op0), rd, [out])
            else:
                P.op(eng, lambda e: e.tensor_scalar(out=out, in0=in0, scalar1=s1, scalar2=s2, op0=op0, op1=op1), rd, [out])

        def stt(out, in0, scalar, in1, op0, op1):
            rd = [in0, in1] + ([] if isinstance(scalar, (int, float)) else [scalar])
            P.op('dve', lambda e: e.scalar_tensor_tensor(out=out, in0=in0, scalar=scalar, in1=in1, op0=op0, op1=op1), rd, [out])

        def cp(eng, out, in_):
            if eng == 'act':
                P.op('act', lambda e: e.copy(out=out, in_=in_), [in_], [out])
            else:
                P.op(eng, lambda e: e.tensor_copy(out=out, in_=in_), [in_], [out])

        def recip(out, in_):
            P.op('dve', lambda e: e.reciprocal(out=out, in_=in_), [in_], [out])

        def san(ap):
            if dbg:
                ts('dve', ap, ap, 1e30, ALU.min, -1e30, ALU.max)

        def scan(out, d0, d1, rd, wr):
            P.op('dve', lambda e: e.tensor_tensor_scan(out=out, data0=d0, data1=d1, initial=0.0, op0=ALU.mult, op1=ALU.add), rd, wr)

        P.op('pool', lambda e: e.memset(ones[:], 1.0), [], [ones[:]])
        P.op('pool', lambda e: e.memset(ident[:], 0.0), [], [ident[:]])
        P.op('pool', lambda e: e.affine_select(out=ident[:], in_=ones[:], pattern=[[-1, 128]], compare_op=ALU.is_equal,
                                               fill=0.0, base=0, channel_multiplier=1), [ones[:]], [ident[:]])
        P.dma('sp', cst[:], consts_d)
        P.dma('sp', flg[:], flags_d)
        chain = flg[:, 0:1]

        for i in range(NT):
            xs = a32((i % 2) * 8192, 2048)
            P.dma('sp', xs, x_in[i * 128:(i + 1) * 128, :])
            for q in range(4):
                bank = psb[(i * 4 + q) % 8]
                for j in range(4):
                    kc = q * 4 + j
                    tr(bank[:, j * 128:(j + 1) * 128], xs[:, kc * 128:(kc + 1) * 128], ident[:])
                eng = 'act' if q % 2 else 'dve'
                cp(eng, xT[:, q * 4:(q + 1) * 4, i * 128:(i + 1) * 128],
                   bank[:].rearrange("p (a b) -> p a b", a=4))

        def load_pv(l):
            for c0 in range(0, NPV, 128):
                n = min(128, NPV - c0)
                st = a32(16384, 128)
                P.dma('sp', st[0:n, :], pvec[l * NPV + c0: l * NPV + c0 + n, :])
                tr(psb[0][:, 0:n], st[0:n, :], ident[0:n, 0:n])
                cp('dve', pvT[:, c0:c0 + n], psb[0][:, 0:n])
            mu = pvT[:, R_MU:R_MU + 21]
            ts('dve', dv2[:, 0:21], mu, -1.0, ALU.mult, 1.0, ALU.add)
            ts('dve', dv2[:, 21:42], mu, 0.5, ALU.mult)
            ts('dve', dv2[:, 42:63], dv2[:, 21:42], flg[:, 1:2], ALU.mult, -1.0, ALU.mult)
            ts('dve', dv2[:, 63:69], pvT[:, R_KA:R_KA + 6], -1.0, ALU.mult, 1.0, ALU.add)
            e0 = dv2[:, 69:81]
            e1 = dv2[:, 81:93]
            act(e0, pvT[:, R_LB:R_LB + 12], AF.Exp)
            act(e1, pvT[:, R_LB + 12:R_LB + 24], AF.Exp)
            ssum = dv2[:, 93:105]
            tt('dve', ssum, e0, e1, ALU.add)
            recip(ssum, ssum)
            tt('dve', e0, e0, ssum, ALU.mult)
            tt('dve', e1, e1, ssum, ALU.mult)
            lbv = dv2[:, 93:105]
            if l == 0:
                tt('dve', lbv, e0, e0, ALU.subtract)
            else:
                tt('dve', e1, e1, e0, ALU.add)
                tt('dve', lbv, e1, e0, ALU.subtract)
            ts('dve', dv2[:, 105:117], lbv, -1.0, ALU.mult, 1.0, ALU.add)

        def compute_mod(l):
            act(scb[:], pvT[:, R_COND:R_COND + 16], AF.Silu)
            psM = psb[1]
            for s in range(36):
                slab = a16(32768 + (s % 2) * 16384, 16, 512)
                P.dma('pool', slab, w_mod[l * D:(l + 1) * D, s * 512:(s + 1) * 512].rearrange("(kc p) n -> p kc n", p=128))
                for cc in range(4):
                    col = s * 4 + cc
                    for k2 in range(KC):
                        mm(psM[:, col:col + 1], slab[:, k2, cc * 128:(cc + 1) * 128], scb[:, k2:k2 + 1],
                           start=(k2 == 0), stop=(k2 == KC - 1))
            tt('dve', modT[:], psM[:, 0:144], pvT[:, R_BMOD:R_BMOD + 144], ALU.add)
            for i in range(3):
                stt(dv[:, i, :], modT[:, (3 * i + 1) * 16:(3 * i + 2) * 16], 1.0,
                    pvT[:, R_NORMG + i * 16:R_NORMG + (i + 1) * 16], ALU.add, ALU.mult)
            ts('dve', dv[:, 3, :], modT[:, 2 * 16:3 * 16], 0.5, ALU.mult)
            cp('dve', dv[:, 4, :], modT[:, 5 * 16:6 * 16])
            ts('dve', dv[:, 5, :], modT[:, 8 * 16:9 * 16], 0.5, ALU.mult)

        RS_OFF = 0
        TMP_OFF = 4096

        def rstd_compute():
            rstd = a32(RS_OFF, T)
            for kc in range(KC):
                sq = a32(TMP_OFF + (kc % 2) * 4096, T)
                act(sq, xT[:, kc, :], AF.Square)
                for half in range(2):
                    mm(psb[2 + half][:], ones[:], sq[:, half * 512:(half + 1) * 512], start=(kc == 0), stop=(kc == KC - 1))
            for half in range(2):
                act(rstd[:, half * 512:(half + 1) * 512], psb[2 + half][:], AF.Sqrt, bias=EPS, scale=1.0 / D)
            recip(rstd, rstd)
            return rstd

        def norm_mod(scale_ap, shift_ap):
            rstd = rstd_compute()
            for kc in range(KC):
                t1 = a32(TMP_OFF + (kc % 2) * 4096, T)
                stt(t1, xT[:, kc, :], scale_ap[:, kc:kc + 1], rstd, ALU.mult, ALU.mult)
                act(hT[:, kc, :], t1, AF.Identity, bias=shift_ap[:, kc:kc + 1])

        F_W13 = 12288
        F_W2 = F_W13 + 32768
        F_ACT = F_W2 + 32768
        F_SIL = F_ACT + 16384

        def ffn(l, f, gate_ap):
            base = (l * 2 + f)
            w1d = ffn_w1[base * D:(base + 1) * D, :]
            w3d = ffn_w3[base * D:(base + 1) * D, :]
            w2d = ffn_w2[base * DFF:(base + 1) * DFF, :]
            cnt = 0
            for grp in range(11):
                s2 = grp % 2
                w2s = a16(F_W2 + s2 * 16384, 4, 2048)
                acb = a16(F_ACT + s2 * 8192, 4, 1024)
                P.dma('pool', w2s, w2d[grp * 512:(grp + 1) * 512, :].rearrange("(c p) d -> p c d", p=128))
                for pr in range(2):
                    pair = grp * 2 + pr
                    s = pair % 2
                    w1s = a16(F_W13 + s * 16384, 16, 256)
                    w3s = a16(F_W13 + s * 16384 + 8192, 16, 256)
                    P.dma('pool', w1s, w1d[:, pair * 256:(pair + 1) * 256].rearrange("(kc p) n -> p kc n", p=128))
                    P.dma('pool', w3s, w3d[:, pair * 256:(pair + 1) * 256].rearrange("(kc p) n -> p kc n", p=128))
                    for c in range(2):
                        for half in range(2):
                            q = cnt % 2
                            cnt += 1
                            g1 = psb[q]
                            g3 = psb[2 + q]
                            hs = slice(half * 512, (half + 1) * 512)
                            for kc in range(KC):
                                mm(g1[:], w1s[:, kc, c * 128:(c + 1) * 128], hT[:, kc, hs], start=(kc == 0), stop=(kc == KC - 1))
                            for kc in range(KC):
                                mm(g3[:], w3s[:, kc, c * 128:(c + 1) * 128], hT[:, kc, hs], start=(kc == 0), stop=(kc == KC - 1))
                            sil = a32(F_SIL + q * 2048, 512)
                            act(sil, g1[:], AF.Silu)
                            tt('dve', acb[:, pr * 2 + c, hs], sil, g3[:], ALU.mult)
                for dc in range(KC):
                    for half in range(2):
                        bank = psb[4 + (dc * 2 + half) % 4]
                        hs = slice(half * 512, (half + 1) * 512)
                        for c in range(4):
                            mm(bank[:], w2s[:, c, dc * 128:(dc + 1) * 128], acb[:, c, hs], start=(c == 0), stop=(c == 3))
                        stt(xT[:, dc, hs], bank[:], gate_ap[:, dc:dc + 1], xT[:, dc, hs], ALU.mult, ALU.add)

        slab_ctr = [0]

        def load_slab(l, col0, ncols):
            st = slab_ctr[0] % 2
            slab_ctr[0] += 1
            slab = a16(M_SLAB + st * 12288, 16, ncols)
            P.dma('pool', slab, w_inp[l * D:(l + 1) * D, col0:col0 + ncols].rearrange("(kc p) n -> p kc n", p=128))
            return slab

        pair_ctr = [0]

        def proj_fm(slab, c):
            pr = pair_ctr[0] % 4
            pair_ctr[0] += 1
            banks = [psb[2 * pr], psb[2 * pr + 1]]
            for half in range(2):
                for kc in range(KC):
                    mm(banks[half][:], slab[:, kc, c * 128:(c + 1) * 128], hT[:, kc, half * 512:(half + 1) * 512],
                       start=(kc == 0), stop=(kc == KC - 1))
            return banks

        wo_ctr = [0]

        def wout_group(l, chunk0, ochs):
            n = len(ochs)
            st = wo_ctr[0] % 2
            wo_ctr[0] += 1
            ws = a16(M_WOUT + st * 8192, 2, 2048)
            P.dma('pool', ws[:, 0:n, :], w_out[l * D + chunk0 * 128: l * D + (chunk0 + n) * 128, :].rearrange("(c p) d -> p c d", p=128))
            for dc in range(KC):
                for half in range(2):
                    bank = psb[4 + (dc * 2 + half) % 4]
                    hs = slice(half * 512, (half + 1) * 512)
                    for c in range(n):
                        mm(bank[:], ws[:, c, dc * 128:(dc + 1) * 128], ochs[c][:, hs], start=(c == 0), stop=(c == n - 1))
                    stt(xT[:, dc, hs], bank[:], dv[:, 4, dc:dc + 1], xT[:, dc, hs], ALU.mult, ALU.add)

        och_ctr = [0]

        def new_och():
            k = och_ctr[0] % 4
            och_ctr[0] += 1
            return a16(M_OCH + k * 2048, T)

        def mixer_a(l):
            S0 = M_SCR
            uT = [a32(S0 + g * 4096, T) for g in range(4)]
            vg = a32(S0 + 16384, 256)
            junk = a32(S0 + 25600, 256)
            vn = a16(S0 + 17920, 128)
            sm = a32(S0 + 18432, 8)
            ngb = a32(S0 + 18944, 512)
            bsb = a32(S0 + 20992, 512)
            wsT = a16(S0 + 23040, 4, 128)
            tmpo = a32(S0 + 24064, 128)
            ochs = [new_och() for _ in range(4)]
            P.dma('sp', ngb, bass.AP(ngb_d.tensor, l * 512, [[0, 128], [1, 512]]))
            P.dma('sp', bsb, bass.AP(bsb_d.tensor, l * 512, [[0, 128], [1, 512]]))
            P.dma('pool', wsT, wsT_d[l * 512:(l + 1) * 512, :].rearrange("(g q) p -> q g p", q=128))
            for s in range(2):
                slab = load_slab(l, s * 256, 256)
                for c in range(2):
                    banks = proj_fm(slab, c)
                    for half in range(2):
                        act(uT[s * 2 + c][:, half * 512:(half + 1) * 512], banks[half][:], AF.Gelu)
            k = 0
            for s in range(2):
                slab = load_slab(l, 512 + s * 256, 256)
                for i in range(NT):
                    tsl = slice(i * 128, (i + 1) * 128)
                    bank = psb[i % 2]
                    for kc in range(KC):
                        mm(bank[:, 0:256], hT[:, kc, tsl], slab[:, kc, :], start=(kc == 0), stop=(kc == KC - 1))
                    act(vg, bank[:, 0:256], AF.Gelu)
                    act(junk, vg, AF.Square)
                    j3 = junk.rearrange("p (a b) -> p a b", a=2)
                    P.op('dve', lambda e, j3=j3: e.tensor_reduce(out=sm[:, 0:2], in_=j3, axis=mybir.AxisListType.X, op=ALU.add),
                         [junk], [sm[:, 0:2]])
                    act(sm[:, 2:4], sm[:, 0:2], AF.Sqrt, bias=EPS, scale=1.0 / 128)
                    recip(sm[:, 4:6], sm[:, 2:4])
                    for c in range(2):
                        g = s * 2 + c
                        vgc = vg[:, c * 128:(c + 1) * 128]
                        stt(vn, vgc, sm[:, 4 + c:5 + c], ngb[:, g * 128:(g + 1) * 128], ALU.mult, ALU.mult)
                        pb = psb[2 + (k % 2)]
                        k += 1
                        mm(pb[:, 0:128], vn, wsT[:, g, :])
                        tt('dve', tmpo, pb[:, 0:128], bsb[:, g * 128:(g + 1) * 128], ALU.add)
                        tt('dve', ochs[g][:, tsl], tmpo, uT[g][:, tsl], ALU.mult)
            if dbg and l == 0:
                for g in range(4):
                    P.dma('pool', dbg_out[:, g * T:(g + 1) * T], ochs[g])
            wout_group(l, 0, ochs[0:2])
            wout_group(l, 2, ochs[2:4])

        def mixer_c(l):
            S0 = M_SCR
            qs = a32(S0, T)
            gs = a16(S0 + 4096, T)
            vtok = a16(S0 + 6144, 8, 128)
            oacc = a32(S0 + 8192, T)
            bA = a32(S0 + 12288, T)
            bB = a32(S0 + 16384, T)
            bC = a32(S0 + 20480, T)
            bD = a32(S0 + 24576, T)
            sgb = a32(S0 + 28672, T)
            vmask = a16(S0 + 32768, 8, 128)
            attm = a16(S0 + 34816, 128)
            ktok = a16(S0 + 35072, 128)
            cdec = a32(S0 + 35328, 64)
            sall = [a32(S0 + 35584 + k * 4096, 8, 128) for k in range(2)]
            tmpS = a32(S0 + 43776, 128)
            sqo = a32(S0 + 44288, T)
            assert S0 + 48384 <= ARENA_B
            ochs = []
            for h in range(6):
                base_c = 3712 + h * 640
                slab1 = load_slab(l, base_c, 384)
                bq = proj_fm(slab1, 0)
                for half in range(2):
                    act(qs[:, half * 512:(half + 1) * 512], bq[half][:], AF.Silu)
                bf = proj_fm(slab1, 1)
                for half in range(2):
                    act(bA[:, half * 512:(half + 1) * 512], bf[half][:], AF.Sigmoid)
                bfb = proj_fm(slab1, 2)
                for half in range(2):
                    act(sgb[:, half * 512:(half + 1) * 512], bfb[half][:], AF.Sigmoid)
                slab2 = load_slab(l, base_c + 384, 256)
                bg = proj_fm(slab2, 0)
                for half in range(2):
                    act(gs[:, half * 512:(half + 1) * 512], bg[half][:], AF.Silu)
                for i4 in range(2):
                    bank = psb[i4]
                    for j in range(4):
                        i = i4 * 4 + j
                        for kc in range(KC):
                            mm(bank[:, j * 128:(j + 1) * 128], hT[:, kc, i * 128:(i + 1) * 128], slab2[:, kc, 128:256],
                               start=(kc == 0), stop=(kc == KC - 1))
                    cp('act', vtok[:, i4 * 4:(i4 + 1) * 4, :], bank[:].rearrange("p (a b) -> p a b", a=4))
                for d in range(2):
                    lb = dv2[:, 93 + d * 6 + h: 94 + d * 6 + h]
                    omlb = dv2[:, 105 + d * 6 + h: 106 + d * 6 + h]
                    src = bA if d == 0 else sgb
                    ts('dve', bA, src, omlb, ALU.mult, lb, ALU.add)
                    act(bB, bA, AF.Ln)
                    ts('dve', bA, bA, -1.0, ALU.mult, 1.0, ALU.add)
                    rm = cst[:, C_RM:C_RM + 128]
                    for i in range(NT):
                        tsl = slice(i * 128, (i + 1) * 128)
                        if d == 0:
                            scan(bC[:, tsl], rm, bB[:, tsl], [rm, bB[:, tsl]], [bC[:, tsl]])
                        else:
                            scan(rev(bC[:, tsl]), rm, rev(bB[:, tsl]), [rm, bB[:, tsl]], [bC[:, tsl]])
                    act(bB, bC, AF.Exp)
                    tt('dve', bB, bB, qs, ALU.mult)
                    act(bD, bC, AF.Exp, scale=-1.0)
                    tt('dve', bD, bD, bA, ALU.mult)
                    eoff = 15 if d == 0 else 0
                    bend = bass.AP(bC.tensor, bC.offset + eoff, [list(bC.ap[0]), [16, 64], [0, 16]])
                    bend1 = bass.AP(bC.tensor, bC.offset + eoff, [list(bC.ap[0]), [16, 64]])
                    act(cdec, bend1, AF.Exp)
                    bC3 = bC.rearrange("p (a b) -> p a b", b=16)
                    sq3 = sqo.rearrange("p (a b) -> p a b", b=16)
                    tt('dve', sq3, bend, bC3, ALU.subtract)
                    act(sqo, sqo, AF.Exp)
                    tt('dve', bC, sqo, bA, ALU.mult)
                    hm = cst[:, C_HMF:C_HMF + 128] if d == 0 else cst[:, C_HMB:C_HMB + 128]
                    bm3 = bass.AP(cst[:].tensor, C_BM, [list(cst[:].ap[0]), [1, 8], [0, 128]])
                    order = list(range(NT)) if d == 0 else list(range(NT - 1, -1, -1))
                    srow = ((l * 2 + d) * 6 + h) * 128
                    for n_i, i in enumerate(order):
                        tsl = slice(i * 128, (i + 1) * 128)
                        tb = n_i % 2
                        if n_i == 0:
                            P.dma('sp', sall[tb][:, 0, :], shg0[srow:srow + 128, :])
                        pa = psb[0]
                        mm(pa[:, 0:128], bD[:, tsl], bB[:, tsl])
                        tt('dve', attm, pa[:, 0:128], hm, ALU.mult)
                        tr(pa[:, 128:256], bC[:, tsl], ident[:])
                        cp('act', ktok, pa[:, 128:256])
                        v3 = bass.AP(vtok.tensor, vtok[:, i, :].offset, [list(vtok.ap[0]), [0, 8], [1, 128]])
                        tt('dve', vmask, v3, bm3, ALU.mult)
                        kv = [psb[1], psb[2]]
                        for q in range(2):
                            mm(kv[q][:], ktok, vmask[:, q * 4:(q + 1) * 4, :])
                        po = psb[3]
                        mm(po[:, 0:128], vtok[:, i, :], attm, start=True, stop=False)
                        corder = list(range(8)) if d == 0 else list(range(7, -1, -1))
                        for s_i, c in enumerate(corder):
                            cur = sall[tb][:, s_i, :]
                            mm(po[:, c * 16:(c + 1) * 16], cur, bB[:, i * 128 + c * 16: i * 128 + (c + 1) * 16],
                               start=False, stop=(s_i == 7))
                            kvc = kv[c // 4][:, (c % 4) * 128:(c % 4 + 1) * 128]
                            seg_end = (s_i == 7 and n_i % 2 == 1)
                            if s_i < 7:
                                nxt = sall[tb][:, s_i + 1, :]
                            elif seg_end:
                                nxt = tmpS
                            else:
                                nxt = sall[1 - tb][:, 0, :]
                            stt(nxt, cur, cdec[:, i * 8 + c: i * 8 + c + 1], kvc, ALU.mult, ALU.add)
                        if n_i % 2 == 1:
                            seg = i // 2
                            orow = (((seg * depth + l) * 2 + d) * 6 + h) * 128
                            P.dma('sp', shg_out[orow:orow + 128, :], tmpS)
                            if n_i < NT - 1:
                                ts('dve', sall[1 - tb][:, 0, :], tmpS, chain, ALU.mult)
                        if d == 0:
                            cp('act', oacc[:, tsl], po[:, 0:128])
                        else:
                            tt('dve', oacc[:, tsl], oacc[:, tsl], po[:, 0:128], ALU.add)
                act(sqo, oacc, AF.Square)
                for half in range(2):
                    mm(psb[4 + half][:], ones[:], sqo[:, half * 512:(half + 1) * 512])
                for half in range(2):
                    act(sqo[:, half * 512:(half + 1) * 512], psb[4 + half][:], AF.Sqrt, bias=EPS, scale=1.0 / 128)
                recip(sqo, sqo)
                stt(oacc, oacc, pvT[:, R_GN:R_GN + 1], sqo, ALU.mult, ALU.mult)
                och = new_och()
                tt('dve', och, oacc, gs, ALU.mult)
                ochs.append(och)
                if dbg and l == 0:
                    P.dma('pool', dbg_out[:, (10 + h) * T:(11 + h) * T], och)
                if h % 2 == 1:
                    wout_group(l, 10 + h - 1, ochs[-2:])

        def mixer_b(l):
            S0 = M_WOUT + 8192
            twd = a16(S0, T)
            adT = a16(S0 + 2048, T)
            sgd = a16(S0 + 4096, T)
            rF = a32(S0 + 6144, T)
            kF = a32(S0 + 10240, T)
            vF = a32(S0 + 14336, T)
            kkF = a32(S0 + 18432, T)
            aF = a32(S0 + 22528, T)
            sgF = a32(S0 + 26624, T)
            yacc = a32(S0 + 30720, T)
            rkd = a32(S0 + 34816, T)
            Q0 = S0 + 38912
            zpad = a32(Q0, 1026)
            zsum = a32(Q0 + 4104, T)
            zm = a32(Q0 + 8200, T)
            cw = a32(Q0, 128)
            w1_ = a32(Q0 + 512, 128)
            w2_ = a32(Q0 + 1024, 128)
            w3_ = a32(Q0 + 1536, 128)
            kd_ = a32(Q0 + 2048, 128)
            b_ = cw
            AR = a32(Q0 + 2560, 2, 128)
            BT = a32(Q0 + 3584, 128)
            KT = a32(Q0 + 4096, 128)
            TOK = a32(Q0 + 4608, 4, 128)
            G1m = [a32(Q0 + 6656 + hh * 1024, 256) for hh in range(2)]
            G2m = [a32(Q0 + 8704 + hh * 1024, 256) for hh in range(2)]
            Xb = [[a32(Q0 + 10752 + hh * 1024 + k * 512, 128) for k in range(2)] for hh in range(2)]
            PQ = [[a32(Q0 + 12800 + hh * 2048 + k * 1024, 256) for k in range(2)] for hh in range(2)]
            MT = a16(Q0 + 16896, 64)
            RT = a16(Q0 + 17152, 128)
            Sst = [a32(Q0 + 17664 + k * 256, 64) for k in range(2)]
            Scb = a16(Q0 + 18176, 64)
            BTb = a16(Q0 + 1024, 128)
            KTb = a16(Q0 + 1280, 128)
            ARb = a16(Q0 + 2048, 2, 128)
            tmp_ = w3_
            assert Q0 + 18304 <= ARENA_B, Q0 + 18304
            wl = a16(Q0 + 12296, 768)
            mu_o = dv2[:, 0:21]
            mu_h = dv2[:, 21:42]
            mu_n = dv2[:, 42:63]

            def shift_mix(banks, mi, out, func=None):
                P.op('pool', lambda e: e.memset(zpad[:, 0:1], 0.0), [], [zpad[:, 0:1]])
                P.op('pool', lambda e: e.memset(zpad[:, 1025:1026], 0.0), [], [zpad[:, 1025:1026]])
                for half in range(2):
                    cp('act', zpad[:, 1 + half * 512: 1 + (half + 1) * 512], banks[half][:])
                tt('dve', zsum, zpad[:, 0:1024], zpad[:, 2:1026], ALU.add)
                act(zm, zpad[:, 1:1025], AF.Identity, scale=mu_o[:, mi:mi + 1])
                stt(zm, zsum, mu_h[:, mi:mi + 1], zm, ALU.mult, ALU.add)
                zc = zpad[:, 1:1025]
                for (dst0, src0) in ((256, 255), (255, 256)):
                    dsts = bass.AP(zm.tensor, zm.offset + dst0, [list(zm.ap[0]), [256, 3]])
                    srcs = bass.AP(zc.tensor, zc.offset + src0, [list(zc.ap[0]), [256, 3]])
                    stt(dsts, srcs, mu_n[:, mi:mi + 1], dsts, ALU.mult, ALU.add)
                if func is None:
                    cp('dve', out, zm)
                else:
                    act(out, zm, func)

            slab = load_slab(l, 1024, 384)
            shift_mix(proj_fm(slab, 0), 0, twd, AF.Tanh)
            shift_mix(proj_fm(slab, 1), 1, adT)
            shift_mix(proj_fm(slab, 2), 2, sgd, AF.Sigmoid)
            ochs = []
            for hp in range(rwk[0]):
                hcols = slice(hp * 128, (hp + 1) * 128)
                slab = load_slab(l, 1408 + hp * 384, 384)
                shift_mix(proj_fm(slab, 0), 3 + hp * 3 + 0, rF)
                shift_mix(proj_fm(slab, 1), 3 + hp * 3 + 1, kF)
                shift_mix(proj_fm(slab, 2), 3 + hp * 3 + 2, vF)
                ts('dve', kkF, kF, pvT[:, R_KK + hp:R_KK + hp + 1], ALU.mult)
                tt('dve', zsum, kkF, kkF, ALU.mult)
                for half in range(2):
                    mm(psb[half][:], cst[:, C_BD64:C_BD64 + 128], zsum[:, half * 512:(half + 1) * 512])
                for half in range(2):
                    act(zsum[:, half * 512:(half + 1) * 512], psb[half][:], AF.Sqrt)
                ts('dve', zsum, zsum, 1e-12, ALU.max)
                recip(zsum, zsum)
                tt('dve', kkF, kkF, zsum, ALU.mult)
                for d in range(rwk[1]):
                    ds = slice(64 * d, 64 * d + 64)
                    P.dma('pool', wl[ds, :], rw2_d[l * 128 + 64 * d: l * 128 + 64 * d + 64, :])
                    bk = [psb[2], psb[3]]
                    for half in range(2):
                        mm(bk[half][:], wl[ds, hcols], twd[ds, half * 512:(half + 1) * 512])
                    for half in range(2):
                        act(sgF[:, half * 512:(half + 1) * 512], bk[half][:], AF.Sigmoid,
                            bias=pvT[:, R_W0 + d * 6 + hp:R_W0 + d * 6 + hp + 1])
                    P.dma('pool', wl[ds, :], ra2_d[l * 128 + 64 * d: l * 128 + 64 * d + 64, :])
                    bk = [psb[4], psb[5]]
                    for half in range(2):
                        mm(bk[half][:], wl[ds, hcols], adT[ds, half * 512:(half + 1) * 512])
                    for half in range(2):
                        act(aF[:, half * 512:(half + 1) * 512], bk[half][:], AF.Sigmoid,
                            bias=pvT[:, R_A0 + d * 6 + hp:R_A0 + d * 6 + hp + 1])
                    order = list(range(NT)) if d == 0 else list(range(NT - 1, -1, -1))
                    MT2 = cst[:, C_MT2F:C_MT2F + 256] if d == 0 else cst[:, C_MT2B:C_MT2B + 256]
                    MS = cst[:, C_SL:C_SL + 128] if d == 0 else cst[:, C_SU:C_SU + 128]
                    srow = ((l * 2 + d) * 12 + 2 * hp) * 64
                    for n_i, i in enumerate(order[:rwk[2]]):
                        tsl = slice(i * 128, (i + 1) * 128)
                        Sc = Sst[n_i % 2]
                        Sn = Sst[1 - n_i % 2]
                        if n_i == 0:
                            P.dma('sp', Sc, srw0[srow:srow + 128, :])
                        if d == 0:
                            scan(cw, ones[:], sgF[:, tsl], [ones[:], sgF[:, tsl]], [cw])
                        else:
                            scan(rev(cw), ones[:], rev(sgF[:, tsl]), [ones[:], sgF[:, tsl]], [cw])
                        act(w1_, cw, AF.Exp, scale=-DECAY_C)
                        act(w2_, cw, AF.Exp, scale=DECAY_C)
                        tt('dve', w3_, cw, sgF[:, tsl], ALU.subtract)
                        act(w3_, w3_, AF.Exp, scale=-DECAY_C)
                        ts('dve', kd_, aF[:, tsl], pvT[:, R_KA + hp:R_KA + hp + 1], ALU.mult, dv2[:, 63 + hp:64 + hp], ALU.add)
                        tt('dve', kd_, kd_, kF[:, tsl], ALU.mult)
                        tt('dve', b_, kkF[:, tsl], aF[:, tsl], ALU.mult)
                        stt(AR[:, 0, :], kkF[:, tsl], -1.0, w3_, ALU.mult, ALU.mult)
                        stt(tmp_, rF[:, tsl], pvT[:, R_RK + hp:R_RK + hp + 1], kd_, ALU.mult, ALU.mult)
                        if d == 0:
                            cp('dve', rkd[:, tsl], tmp_)
                        else:
                            tt('dve', rkd[:, tsl], rkd[:, tsl], tmp_, ALU.add)
                        tt('dve', AR[:, 1, :], rF[:, tsl], w1_, ALU.mult)
                        tt('dve', BT, b_, w2_, ALU.mult)
                        tt('dve', KT, kd_, w2_, ALU.mult)
                        if rwk[3] < 1:
                            continue
                        pt = psb[0]
                        _rwx = ''
                        if 'a' not in _rwx:
                            mm(pt[:, 0:128], AR[:, 0, :], ident[:])
                        if 'b' not in _rwx:
                            mm(pt[:, 128:256], BT, ident[:])
                        if 'k' not in _rwx:
                            mm(pt[:, 256:384], KT, ident[:])
                        if 'v' not in _rwx:
                            mm(pt[:, 384:512], vF[:, tsl], ident[:])
                        if 'c' not in _rwx:
                            cp('act', TOK, pt[:].rearrange("p (a b) -> p a b", a=4))
                        if 'd' in _rwx:
                            P.dma('sp', dbg_out[:, 0:256], AR.rearrange("p a b -> p (a b)"))
                            P.dma('sp', dbg_out[:, 256:384], BT)
                            P.dma('sp', dbg_out[:, 384:512], KT)
                            P.dma('sp', dbg_out[:, 512:640], cw)
                            P.dma('sp', dbg_out[:, 640:768], w2_)
                            P.dma('sp', dbg_out[:, 1024:2048], sgF)
                            P.dma('sp', dbg_out[:, 2048:3072], kkF)
                            P.dma('sp', dbg_out[:, 3072:4096], aF)
                        if rwk[3] < 2:
                            continue
                        if 'p' not in _rwx:
                            cp('pool', ARb, AR)
                            cp('pool', BTb, BT)
                            cp('pool', KTb, KT)
                            cp('pool', Scb, Sc)
                        pG = [psb[1], psb[2]]
                        pB = [psb[3], psb[4]]
                        hsl = [slice(0, 64), slice(64, 128)]
                        for hh in range(2):
                            hs = hsl[hh]
                            if 'g' not in _rwx:
                                mm(pG[hh][:, 0:256], BTb[hs, :], ARb[hs, :, :])
                                mm(pG[hh][:, 256:512], KTb[hs, :], ARb[hs, :, :])
                            if '3' not in _rwx:
                                mm(pB[hh][:, 0:128], ARb[hs, 0, :], BTb[hs, :])
                            if 'm' not in _rwx:
                                if '6' not in _rwx:
                                    tt('dve', G2m[hh], pG[hh][:, 256:512], MT2, ALU.mult)
                                if '5' not in _rwx:
                                    tt('dve', G1m[hh], pG[hh][:, 0:256], MT2, ALU.mult)
                                if '7' not in _rwx:
                                    tt('dve', PQ[hh][0][:, 128:256], pB[hh][:, 0:128], MS, ALU.mult)
                                if '8' not in _rwx:
                                    cp('act', PQ[hh][0][:, 0:128], G1m[hh][:, 0:128])
                            if 'q' not in _rwx:
                                mm(pB[hh][:, 128:192], G2m[hh][:, 0:128], TOK[:, 3, hs])
                                cp('act', Xb[hh][0][:, 0:64], TOK[:, 0, hs])
                                cp('act', Xb[hh][0][:, 64:128], pB[hh][:, 128:192])
                        if rwk[3] < 3:
                            continue
                        pC = [psb[5], psb[6]]
                        for lvl in range(7):
                            for hh in range(2):
                                Pm = PQ[hh][lvl % 2][:, 0:128]
                                Qm = PQ[hh][lvl % 2][:, 128:256]
                                Xc = Xb[hh][lvl % 2]
                                Xn = Xb[hh][1 - lvl % 2]
                                mm(pC[hh][:, 0:128], Pm, Xc)
                                tt('dve', Xn, Xc, pC[hh][:, 0:128], ALU.add)
                                if lvl < 6:
                                    mm(pC[hh][:, 128:256], Qm, Pm)
                                    mm(pC[hh][:, 256:384], Pm, Qm)
                                    cp('act', PQ[hh][1 - lvl % 2], pC[hh][:, 128:384])
                        if rwk[3] < 4:
                            continue
                        pS = psb[7]
                        pE = psb[3]
                        pY = psb[4]
                        for hh in range(2):
                            hs = hsl[hh]
                            Xf = Xb[hh][1]
                            mm(pE[hs, 0:64], Xf[:, 0:64], TOK[:, 1, hs])
                            tt('dve', MT[hs, :], pE[hs, 0:64], ident[hs, hs], ALU.add)
                            mm(pE[hs, 64:192], Xf[:, 0:64], G1m[hh][:, 128:256])
                            tt('dve', RT[hs, :], pE[hs, 64:192], AR[hs, 1, :], ALU.add)
                            mm(pS[hs, 0:64], TOK[:, 1, hs], Xf[:, 64:128], start=True, stop=False)
                            mm(pS[hs, 0:64], TOK[:, 2, hs], TOK[:, 3, hs], start=False, stop=False)
                            mm(pS[hs, 0:64], MT[hs, :], Scb[hs, :], start=False, stop=True)
                            mm(pY[hs, 0:128], Scb[hs, :], RT[hs, :], start=True, stop=False)
                            mm(pY[hs, 0:128], Xf[:, 64:128], G1m[hh][:, 128:256], start=False, stop=False)
                            mm(pY[hs, 0:128], TOK[:, 3, hs], G2m[hh][:, 128:256], start=False, stop=True)
                        if d == 0:
                            cp('act', yacc[:, tsl], pY[:, 0:128])
                        else:
                            tt('dve', yacc[:, tsl], yacc[:, tsl], pY[:, 0:128], ALU.add)
                        wend = w1_[:, 127:128] if d == 0 else w1_[:, 0:1]
                        if n_i % 2 == 1:
                            ts('dve', tmp_[:, 0:64], pS[:, 0:64], wend, ALU.mult)
                            san(tmp_[:, 0:64])
                            seg = i // 2
                            orow = (((seg * depth + l) * 2 + d) * 12 + 2 * hp) * 64
                            P.dma('sp', srw_out[orow:orow + 128, :], tmp_[:, 0:64])
                            if n_i < NT - 1:
                                ts('dve', Sn, tmp_[:, 0:64], chain, ALU.mult)
                        else:
                            ts('dve', Sn, pS[:, 0:64], wend, ALU.mult)
                bd = cst[:, C_BD64:C_BD64 + 128]
                for half in range(2):
                    mm(psb[half][:], bd, yacc[:, half * 512:(half + 1) * 512])
                for half in range(2):
                    hsz = slice(half * 512, (half + 1) * 512)
                    stt(yacc[:, hsz], psb[half][:], -1.0 / 64, yacc[:, hsz], ALU.mult, ALU.add)
                tt('dve', aF, yacc, yacc, ALU.mult)
                for half in range(2):
                    mm(psb[2 + half][:], bd, aF[:, half * 512:(half + 1) * 512])
                for half in range(2):
                    act(aF[:, half * 512:(half + 1) * 512], psb[2 + half][:], AF.Sqrt, bias=LNX_EPS, scale=1.0 / 64)
                recip(aF, aF)
                stt(yacc, yacc, pvT[:, R_LNG + hp:R_LNG + hp + 1], aF, ALU.mult, ALU.mult)
                for half in range(2):
                    mm(psb[4 + half][:], bd, rkd[:, half * 512:(half + 1) * 512])
                for half in range(2):
                    hsz = slice(half * 512, (half + 1) * 512)
                    tt('dve', aF[:, hsz], psb[4 + half][:], vF[:, hsz], ALU.mult)
                stt(yacc, yacc, pvT[:, R_LNB + hp:R_LNB + hp + 1], aF, ALU.add, ALU.add)
                P.dma('pool', wl[:, :], rg2_d[l * 128:(l + 1) * 128, :])
                och = new_och()
                for half in range(2):
                    mm(psb[6 + half][:], wl[:, hcols], sgd[:, half * 512:(half + 1) * 512])
                for half in range(2):
                    hsz = slice(half * 512, (half + 1) * 512)
                    tt('dve', och[:, hsz], yacc[:, hsz], psb[6 + half][:], ALU.mult)
                ochs.append(och)
                if dbg and l == 0:
                    san(och)
                    P.dma('pool', dbg_out[:, (4 + hp) * T:(5 + hp) * T], och)
                    san(yacc)
                    P.dma('sp', dbg_out[:, 0:T], yacc)
                if hp % 2 == 1:
                    wo_ctr[0] = 0
                    wout_group(l, 4 + hp - 1, ochs[-2:])

        for l in range(depth):
            load_pv(l)
            if 'mod' in parts:
                compute_mod(l)
            else:
                P.op('pool', lambda e: e.memset(modT[:], 0.0), [], [modT[:]])
                for i in range(3):
                    cp('dve', dv[:, i, :], pvT[:, R_NORMG + i * 16:R_NORMG + (i + 1) * 16])
                P.op('pool', lambda e: e.memset(dv[:, 3:6, :], 1.0), [], [dv[:, 3:6, :]])
            if 'ffn1' in parts:
                norm_mod(dv[:, 0, :], modT[:, 0:16])
                ffn(l, 0, dv[:, 3, :])
            if any(m in parts for m in 'abc'):
                norm_mod(dv[:, 1, :], modT[:, 3 * 16:4 * 16])
                if 'a' in parts:
                    mixer_a(l)
                if 'b' in parts:
                    mixer_b(l)
                if 'c' in parts:
                    mixer_c(l)
            if 'ffn2' in parts:
                norm_mod(dv[:, 2, :], modT[:, 6 * 16:7 * 16])
                ffn(l, 1, dv[:, 5, :])

        rstd = rstd_compute()
        fg = pvT[:, R_FG:R_FG + 16]
        for i in range(NT):
            ys = a32(16384 + (i % 2) * 8192, 2048)
            tsl = slice(i * 128, (i + 1) * 128)
            for q in range(4):
                bank = psb[4 + (i * 4 + q) % 4]
                for j in range(4):
                    kc = q * 4 + j
                    t1 = a32(40960 + (kc % 4) * 512, 128)
                    stt(t1, xT[:, kc, tsl], fg[:, kc:kc + 1], rstd[:, tsl], ALU.mult, ALU.mult)
                    tr(bank[:, j * 128:(j + 1) * 128], t1, ident[:])
                cp('act', ys[:, q * 512:(q + 1) * 512], bank[:])
            san(ys)
            P.dma('sp', y_out[tsl, :], ys)
        P.final_wait_all('sp')
        P.build()
    return nc


def _perm_cols():
    A = list(range(1024))
    B0 = 1024
    C0 = 3712
    lora = list(range(B0 + 2304, B0 + 2688))
    b = []
    for hp in range(6):
        for part in range(3):
            b += list(range(B0 + part * 768 + hp * 128, B0 + part * 768 + (hp + 1) * 128))
    c = []
    for h in range(6):
        for part in (0, 1, 2, 4, 3):
            c += list(range(C0 + part * 768 + h * 128, C0 + part * 768 + (h + 1) * 128))
    return np.array(A + lora + b + c)


def _consts():
    j = np.arange(128)[:, None]
    t = np.arange(128)[None, :]
    su = (t > j).astype(np.float32)
    iu = (t >= j).astype(np.float32)
    sl = (t < j).astype(np.float32)
    il = (t <= j).astype(np.float32)
    same = (j // 16 == t // 16)
    hmf = (same & (j <= t)).astype(np.float32)
    hmb = (same & (j >= t)).astype(np.float32)
    bd = (j // 64 == t // 64).astype(np.float32)
    bm = (j // 16 == np.arange(8)[None, :]).astype(np.float32)
    rm = np.broadcast_to((np.arange(128) % 16 != 0).astype(np.float32)[None, :], (128, 128))
    return np.ascontiguousarray(np.concatenate([su, iu, sl, il, sl, su, hmf, hmb, bd, bm, rm], axis=1))


def _pack_pvec(I, cond, depth):
    rows = np.zeros((depth, NPV, 128), np.float32)
    mu_order = [18, 19, 20] + [part * 6 + hp for hp in range(6) for part in range(3)]
    for l in range(depth):
        r = rows[l]
        r[R_NORMG:R_NORMG + 48] = I['norm_g'][l].reshape(48, 128)
        r[R_BMOD:R_BMOD + 144] = I['b_mod'][l].reshape(144, 128)
        r[R_MU:R_MU + 21] = I['rwkv_mu'][l].reshape(21, 128)[mu_order]
        r[R_W0:R_W0 + 12] = I['rwkv_w0'][l].reshape(12, 128)
        r[R_A0:R_A0 + 12] = I['rwkv_a0'][l].reshape(12, 128)
        r[R_KK:R_KK + 6] = I['rwkv_kk'][l].reshape(6, 128)
        r[R_KA:R_KA + 6] = I['rwkv_ka'][l].reshape(6, 128)
        r[R_RK:R_RK + 6] = I['rwkv_rk'][l].reshape(6, 128)
        r[R_LNG:R_LNG + 6] = I['rwkv_lnx_g'][l].reshape(6, 128)
        r[R_LNB:R_LNB + 6] = I['rwkv_lnx_b'][l].reshape(6, 128)
        r[R_LB:R_LB + 24] = I['hgrn_lb'].reshape(24, 128)
        r[R_GN] = I['hgrn_gn'][l]
        r[R_FG:R_FG + 16] = I['final_g'].reshape(16, 128)
        r[R_COND:R_COND + 16] = cond.reshape(16, 128)
    return rows.reshape(depth * NPV, 128)


def _shared_inputs(I, depth):
    perm = _perm_cols()
    return dict(
        consts=_consts(),
        w_mod=I['w_mod'][:depth].reshape(depth * D, 9 * D),
        ffn_w1=I['ffn_w1'][:depth].reshape(depth * 2 * D, DFF),
        ffn_w3=I['ffn_w3'][:depth].reshape(depth * 2 * D, DFF),
        ffn_w2=I['ffn_w2'][:depth].reshape(depth * 2 * DFF, D),
        w_inp=np.ascontiguousarray(I['w_in'][:depth][:, :, perm]).reshape(depth * D, IN_COLS),
        w_out=I['w_out'][:depth].reshape(depth * D, D),
        wsT=np.ascontiguousarray(np.swapaxes(I['mlp_ws'][:depth], -1, -2)).reshape(depth * 512, 128),
        ngb=I['mlp_norm_g'][:depth].reshape(depth, 512),
        bsb=I['mlp_bs'][:depth].reshape(depth, 512),
        rw2=I['rwkv_w2'][:depth].reshape(depth * 128, 768),
        ra2=I['rwkv_a2'][:depth].reshape(depth * 128, 768),
        rg2=I['rwkv_g2'][:depth].reshape(depth * 128, 768),
    )


def _core_inputs(I, core, depth, shared):
    m = dict(shared)
    if core < 4:
        m['x_in'] = np.ascontiguousarray(I['x_prompt'][4 * core:4 * core + 4].reshape(T, D))
        cond = I['c_ctx']
        m['flags'] = np.tile(np.array([[0.0, 1.0]], np.float32), (128, 1))
        m['srw0'] = np.zeros((depth * 2 * 12 * 64, 64), np.float32)
        m['shg0'] = np.zeros((depth * 2 * 6 * 128, 128), np.float32)
    else:
        b = core - 4
        m['x_in'] = np.ascontiguousarray(I['x_sample'][b])
        cond = I['c'][b]
        m['flags'] = np.tile(np.array([[1.0, 0.0]], np.float32), (128, 1))
        m['srw0'] = np.ascontiguousarray(np.swapaxes(I['state_rwkv'][b, :depth], -1, -2)).reshape(depth * 2 * 12 * 64, 64)
        m['shg0'] = np.ascontiguousarray(I['state_hgrn'][b, :depth]).reshape(depth * 2 * 6 * 128, 128)
    m['pvec'] = _pack_pvec(I, cond, depth)
    return m


def _unpack_states(R, depth):
    rw = R['srw_out'].reshape(4, depth, 2, 12, 64, 64)
    rw = np.ascontiguousarray(np.swapaxes(rw, -1, -2))
    hg = R['shg_out'].reshape(4, depth, 2, 6, 128, 128)
    return rw, hg


_NC_CACHE = {}


def kernel(x_prompt, x_sample, state_rwkv, state_hgrn, c, c_ctx, norm_g, w_mod, b_mod,
           ffn_w1, ffn_w3, ffn_w2, w_in, w_out, mlp_norm_g, mlp_ws, mlp_bs, rwkv_mu,
           rwkv_w0, rwkv_w2, rwkv_a0, rwkv_a2, rwkv_g2, rwkv_kk, rwkv_ka, rwkv_rk,
           rwkv_lnx_g, rwkv_lnx_b, hgrn_lb, hgrn_gn, final_g):
    I = {k: np.asarray(v, np.float32) for k, v in dict(
        x_prompt=x_prompt, x_sample=x_sample, state_rwkv=state_rwkv, state_hgrn=state_hgrn, c=c, c_ctx=c_ctx,
        norm_g=norm_g, w_mod=w_mod, b_mod=b_mod, ffn_w1=ffn_w1, ffn_w3=ffn_w3, ffn_w2=ffn_w2, w_in=w_in,
        w_out=w_out, mlp_norm_g=mlp_norm_g, mlp_ws=mlp_ws, mlp_bs=mlp_bs, rwkv_mu=rwkv_mu, rwkv_w0=rwkv_w0,
        rwkv_w2=rwkv_w2, rwkv_a0=rwkv_a0, rwkv_a2=rwkv_a2, rwkv_g2=rwkv_g2, rwkv_kk=rwkv_kk, rwkv_ka=rwkv_ka,
        rwkv_rk=rwkv_rk, rwkv_lnx_g=rwkv_lnx_g, rwkv_lnx_b=rwkv_lnx_b, hgrn_lb=hgrn_lb, hgrn_gn=hgrn_gn,
        final_g=final_g).items()}
    if 'nc' not in _NC_CACHE:
        _NC_CACHE['nc'] = build_program()
    nc = _NC_CACHE['nc']
    shared = _shared_inputs(I, DEPTH)
    in_maps = [_core_inputs(I, core, DEPTH, shared) for core in range(6)]
    res = run_bass_kernel_spmd(nc, in_maps, core_ids=list(range(6)))
    R = res.results
    y_prompt = np.stack([R[cc]['y_out'].reshape(4, 256, D) for cc in range(4)]).reshape(16, 256, D)
    y_sample = np.stack([R[4]['y_out'], R[5]['y_out']])
    rws, hgs = [], []
    for cc in range(4):
        rw, hg = _unpack_states(R[cc], DEPTH)
        rws.append(rw)
        hgs.append(hg)
    new_rw = np.concatenate(rws, axis=0)
    new_hg = np.concatenate(hgs, axis=0)
    return (np.ascontiguousarray(y_prompt, np.float32), np.ascontiguousarray(y_sample, np.float32),
            np.ascontiguousarray(new_rw, np.float32), np.ascontiguousarray(new_hg, np.float32))
```

```python
import numpy as np
from contextlib import ExitStack
import concourse.bass as bass
import concourse.mybir as mybir
from concourse.bass_utils import run_bass_kernel_spmd

F32 = mybir.dt.float32
BF16 = mybir.dt.bfloat16
AF = mybir.ActivationFunctionType
ALU = mybir.AluOpType

EPOCH = 16000
NDMA = 24

D = 2048
KC = 16
T = 1024
NT = 8
DFF = 5632
NFC = 44
DEPTH = 2
IN_COLS = 7552
EPS = 1e-6
LNX_EPS = 64e-5
NPV = 324
ARENA_B = 96 * 1024


def _dsize(dt):
    return 2 if dt == BF16 else 4


class Prog:
    ENGS = ('pe', 'act', 'dve', 'pool', 'sp')

    def __init__(self, nc):
        self.nc = nc
        self.streams = {e: [] for e in self.ENGS}
        self.cnt = {e: 0 for e in ('pe', 'act', 'dve', 'pool')}
        self.seen = {e: {} for e in self.ENGS}
        self.recs = {}
        self.dma_cnt = [0] * NDMA
        self.dma_rr = 0
        self.semkeys = set()

    def _region(self, ap):
        t = ap.tensor
        name = t.name
        dims = ap.ap
        off = ap.offset
        sp = str(ap.space)
        es = _dsize(ap.dtype)
        if 'PSUM' in sp:
            return name, (0, 128, 0, 2048)
        if 'SB' in sp:
            row = 1
            for s in t.shape[1:]:
                row *= s
            p0 = off // row
            f0 = off % row
            pc = dims[0][1]
            lo = f0
            hi = f0
            for st, c in dims[1:]:
                if st < 0:
                    lo += st * (c - 1)
                else:
                    hi += st * (c - 1)
            return name, (p0, p0 + pc, lo * es, (hi + 1) * es)
        lo = off
        hi = off
        for st, c in dims:
            if st < 0:
                lo += st * (c - 1)
            else:
                hi += st * (c - 1)
        return name, (0, 1, lo * es, (hi + 1) * es)

    @staticmethod
    def _ov(a, b):
        return a[0] < b[1] and b[0] < a[1] and a[2] < b[3] and b[2] < a[3]

    @staticmethod
    def _contains(a, b):
        return a[0] <= b[0] and b[1] <= a[1] and a[2] <= b[2] and b[3] <= a[3]

    def _deps(self, eng, reads, writes):
        waits = {}
        acc = [(self._region(a), False) for a in reads] + [(self._region(a), True) for a in writes]
        for (name, rg), isw in acc:
            for (r_rg, r_w, r_key, r_val, r_eng) in self.recs.get(name, ()):
                if not (r_w or isw):
                    continue
                if not self._ov(rg, r_rg):
                    continue
                if eng == 'pe' and r_eng == 'pe':
                    continue
                if self.seen[eng].get(r_key, 0) >= r_val:
                    continue
                if waits.get(r_key, 0) < r_val:
                    waits[r_key] = r_val
        for k, v in waits.items():
            self.seen[eng][k] = v
        return acc, list(waits.items())

    def _record(self, acc, key, val, eng):
        for (name, rg), isw in acc:
            lst = self.recs.setdefault(name, [])
            if isw:
                lst[:] = [r for r in lst if not self._contains(rg, r[0])]
            elif key[0] != 'dma':
                lst[:] = [r for r in lst if not (r[2][0] == key[0] and (not r[1]) and r[4] == eng
                                                 and self._contains(rg, r[0]))]
            lst.append((rg, isw, key, val, eng))

    def op(self, eng, fn, reads=(), writes=()):
        acc, waits = self._deps(eng, reads, writes)
        n = self.cnt[eng]
        self.cnt[eng] = n + 1
        key = (eng, n // EPOCH)
        val = n % EPOCH + 1
        self.semkeys.add(key)
        self.streams[eng].append((fn, waits, (key, 1)))
        self._record(acc, key, val, eng)

    def dma(self, eng, out, in_, **kw):
        j = self.dma_rr
        self.dma_rr = (j + 1) % NDMA
        key = ('dma', j)
        self.semkeys.add(key)
        acc, waits = self._deps(eng, [in_], [out])
        prev = self.dma_cnt[j] * 16
        wd = dict(waits)
        if prev > 0 and self.seen[eng].get(key, 0) < prev:
            wd[key] = max(wd.get(key, 0), prev)
            self.seen[eng][key] = prev
        self.dma_cnt[j] += 1
        val = self.dma_cnt[j] * 16
        self.streams[eng].append((lambda e: e.dma_start(out=out, in_=in_, **kw), list(wd.items()), (key, 16)))
        self._record(acc, key, val, eng)

    def final_wait_all(self, eng='sp'):
        waits = []
        for j in range(NDMA):
            if self.dma_cnt[j]:
                waits.append((('dma', j), self.dma_cnt[j] * 16))
        for e, n in self.cnt.items():
            if n:
                waits.append(((e, (n - 1) // EPOCH), (n - 1) % EPOCH + 1))
        self.streams[eng].append((None, waits, None))

    def build(self):
        nc = self.nc
        keys = sorted(self.semkeys, key=str)
        with ExitStack() as es:
            sems = {k: es.enter_context(nc.semaphore("s_%s_%d" % (k[0], k[1]))) for k in keys}
            block = es.enter_context(nc.Block())

            def emit(stream):
                def body(e):
                    for fn, waits, inc in stream:
                        for k, v in waits:
                            e.wait_ge(sems[k], v)
                        if fn is None:
                            continue
                        ins = fn(e)
                        if inc is not None:
                            ins.then_inc(sems[inc[0]], inc[1])
                return body

            if self.streams['sp']:
                block.sync(emit(self.streams['sp']))
            if self.streams['pe']:
                block.tensor(emit(self.streams['pe']))
            if self.streams['act']:
                block.scalar(emit(self.streams['act']))
            if self.streams['dve']:
                block.vector(emit(self.streams['dve']))
            if self.streams['pool']:
                block.gpsimd(emit(self.streams['pool']))


R_NORMG = 0
R_BMOD = 48
R_MU = 192
R_W0 = 213
R_A0 = 225
R_KK = 237
R_KA = 243
R_RK = 249
R_LNG = 255
R_LNB = 261
R_LB = 267
R_GN = 291
R_FG = 292
R_COND = 308

C_MT2F = 0
C_MT2B = 256
C_SL = 512
C_SU = 640
C_HMF = 768
C_HMB = 896
C_BD64 = 1024
C_BM = 1152
C_RM = 1160
NCONST = 1288

M_SLAB = 0
M_OCH = 24576
M_WOUT = 32768
M_SCR = 49152
DECAY_C = 0.6065306597126334

ALL_PARTS = ('mod', 'ffn1', 'a', 'b', 'c', 'ffn2')


def build_program(depth=DEPTH, parts=ALL_PARTS, dbg=False, rwk=(6, 2, 8, 9)):
    nc = bass.Bass("TRN2", target_bir_lowering=False)
    dram_in = lambda name, shape: nc.dram_tensor(name, list(shape), F32, kind="ExternalInput").ap()
    dram_out = lambda name, shape: nc.dram_tensor(name, list(shape), F32, kind="ExternalOutput").ap()
    x_in = dram_in("x_in", [T, D])
    pvec = dram_in("pvec", [depth * NPV, 128])
    consts_d = dram_in("consts", [128, NCONST])
    flags_d = dram_in("flags", [128, 2])
    srw0 = dram_in("srw0", [depth * 2 * 12 * 64, 64])
    shg0 = dram_in("shg0", [depth * 2 * 6 * 128, 128])
    w_mod = dram_in("w_mod", [depth * D, 9 * D])
    ffn_w1 = dram_in("ffn_w1", [depth * 2 * D, DFF])
    ffn_w3 = dram_in("ffn_w3", [depth * 2 * D, DFF])
    ffn_w2 = dram_in("ffn_w2", [depth * 2 * DFF, D])
    w_inp = dram_in("w_inp", [depth * D, IN_COLS])
    w_out = dram_in("w_out", [depth * D, D])
    wsT_d = dram_in("wsT", [depth * 4 * 128, 128])
    ngb_d = dram_in("ngb", [depth, 512])
    bsb_d = dram_in("bsb", [depth, 512])
    rw2_d = dram_in("rw2", [depth * 128, 768])
    ra2_d = dram_in("ra2", [depth * 128, 768])
    rg2_d = dram_in("rg2", [depth * 128, 768])
    y_out = dram_out("y_out", [T, D])
    srw_out = dram_out("srw_out", [4 * depth * 2 * 12 * 64, 64])
    shg_out = dram_out("shg_out", [4 * depth * 2 * 6 * 128, 128])
    dbg_out = dram_out("dbg_out", [128, 16 * T]) if dbg else None

    P = Prog(nc)
    with ExitStack() as es:
        sb = lambda name, shape, dt=F32: es.enter_context(nc.sbuf_tensor(name, list(shape), dt))
        xT = sb("xT", [128, KC, T])
        hT = sb("hT", [128, KC, T], BF16)
        arena = sb("arena", [128, ARENA_B // 4])
        pvT = sb("pvT", [128, NPV])
        modT = sb("modT", [128, 144])
        dv = sb("dv", [128, 6, KC])
        dv2 = sb("dv2", [128, 120])
        scb = sb("scb", [128, KC], BF16)
        ident = sb("ident", [128, 128])
        ones = sb("ones", [128, 128])
        cst = sb("cst", [128, NCONST])
        flg = sb("flg", [128, 2])
        psb = [es.enter_context(nc.psum_tensor("ps%d" % i, [128, 512], F32)) for i in range(8)]
        ar32 = arena[:]
        ar16 = arena[:].bitcast(BF16)

        def a32(off_b, *shape):
            n = int(np.prod(shape))
            assert off_b % 4 == 0 and off_b + 4 * n <= ARENA_B, (off_b, shape)
            ap = ar32[:, off_b // 4: off_b // 4 + n]
            if len(shape) == 2:
                ap = ap.rearrange("p (a b) -> p a b", a=shape[0])
            return ap

        def a16(off_b, *shape):
            n = int(np.prod(shape))
            assert off_b % 4 == 0 and off_b + 2 * n <= ARENA_B, (off_b, shape)
            ap = ar16[:, off_b // 2: off_b // 2 + n]
            if len(shape) == 2:
                ap = ap.rearrange("p (a b) -> p a b", a=shape[0])
            return ap

        def rev(ap):
            n = ap.ap[-1][1]
            assert len(ap.ap) == 2 and ap.ap[-1][0] == 1
            return bass.AP(ap.tensor, ap.offset + n - 1, [list(ap.ap[0]), [-1, n]])

        def mm(out, lhsT, rhs, start=True, stop=True):
            P.op('pe', lambda e: e.matmul(out, lhsT=lhsT, rhs=rhs, start=start, stop=stop), [lhsT, rhs], [out])

        def tr(out, in_, idn):
            P.op('pe', lambda e: e.transpose(out=out, in_=in_, identity=idn), [in_, idn], [out])

        def act(out, in_, func, bias=None, scale=None, accum=None):
            kw = {}
            rd = [in_]
            wr = [out]
            if bias is not None:
                kw['bias'] = bias
                if not isinstance(bias, (int, float)):
                    rd.append(bias)
            if scale is not None:
                kw['scale'] = scale
                if not isinstance(scale, (int, float)):
                    rd.append(scale)
            if accum is not None:
                kw['accum_out'] = accum
                wr.append(accum)
            P.op('act', lambda e: e.activation(out=out, in_=in_, func=func, **kw), rd, wr)

        def tt(eng, out, in0, in1, op):
            P.op(eng, lambda e: e.tensor_tensor(out=out, in0=in0, in1=in1, op=op), [in0, in1], [out])

        def ts(eng, out, in0, s1, op0, s2=None, op1=None):
            rd = [in0] + [s for s in (s1, s2) if s is not None and not isinstance(s, (int, float))]
            if op1 is None:
                P.op(eng, lambda e: e.tensor_scalar(out=out, in0=in0, scalar1=s1, scalar2=None, op0=op0), rd, [out])
            else:
                P.op(eng, lambda e: e.tensor_scalar(out=out, in0=in0, scalar1=s1, scalar2=s2, op0=op0, op1=op1), rd, [out])

        def stt(out, in0, scalar, in1, op0, op1):
            rd = [in0, in1] + ([] if isinstance(scalar, (int, float)) else [scalar])
            P.op('dve', lambda e: e.scalar_tensor_tensor(out=out, in0=in0, scalar=scalar, in1=in1, op0=op0, op1=op1), rd, [out])

        def cp(eng, out, in_):
            if eng == 'act':
                P.op('act', lambda e: e.copy(out=out, in_=in_), [in_], [out])
            else:
                P.op(eng, lambda e: e.tensor_copy(out=out, in_=in_), [in_], [out])

        def recip(out, in_):
            P.op('dve', lambda e: e.reciprocal(out=out, in_=in_), [in_], [out])

        def san(ap):
            if dbg:
                ts('dve', ap, ap, 1e30, ALU.min, -1e30, ALU.max)

        def scan(out, d0, d1, rd, wr):
            P.op('dve', lambda e: e.tensor_tensor_scan(out=out, data0=d0, data1=d1, initial=0.0, op0=ALU.mult, op1=ALU.add), rd, wr)

        P.op('pool', lambda e: e.memset(ones[:], 1.0), [], [ones[:]])
        P.op('pool', lambda e: e.memset(ident[:], 0.0), [], [ident[:]])
        P.op('pool', lambda e: e.affine_select(out=ident[:], in_=ones[:], pattern=[[-1, 128]], compare_op=ALU.is_equal,
                                               fill=0.0, base=0, channel_multiplier=1), [ones[:]], [ident[:]])
        P.dma('sp', cst[:], consts_d)
        P.dma('sp', flg[:], flags_d)
        chain = flg[:, 0:1]

        for i in range(NT):
            xs = a32((i % 2) * 8192, 2048)
            P.dma('sp', xs, x_in[i * 128:(i + 1) * 128, :])
            for q in range(4):
                bank = psb[(i * 4 + q) % 8]
                for j in range(4):
                    kc = q * 4 + j
                    tr(bank[:, j * 128:(j + 1) * 128], xs[:, kc * 128:(kc + 1) * 128], ident[:])
                eng = 'act' if q % 2 else 'dve'
                cp(eng, xT[:, q * 4:(q + 1) * 4, i * 128:(i + 1) * 128],
                   bank[:].rearrange("p (a b) -> p a b", a=4))

        def load_pv(l):
            for c0 in range(0, NPV, 128):
                n = min(128, NPV - c0)
                st = a32(16384, 128)
                P.dma('sp', st[0:n, :], pvec[l * NPV + c0: l * NPV + c0 + n, :])
                tr(psb[0][:, 0:n], st[0:n, :], ident[0:n, 0:n])
                cp('dve', pvT[:, c0:c0 + n], psb[0][:, 0:n])
            mu = pvT[:, R_MU:R_MU + 21]
            ts('dve', dv2[:, 0:21], mu, -1.0, ALU.mult, 1.0, ALU.add)
            ts('dve', dv2[:, 21:42], mu, 0.5, ALU.mult)
            ts('dve', dv2[:, 42:63], dv2[:, 21:42], flg[:, 1:2], ALU.mult, -1.0, ALU.mult)
            ts('dve', dv2[:, 63:69], pvT[:, R_KA:R_KA + 6], -1.0, ALU.mult, 1.0, ALU.add)
            e0 = dv2[:, 69:81]
            e1 = dv2[:, 81:93]
            act(e0, pvT[:, R_LB:R_LB + 12], AF.Exp)
            act(e1, pvT[:, R_LB + 12:R_LB + 24], AF.Exp)
            ssum = dv2[:, 93:105]
            tt('dve', ssum, e0, e1, ALU.add)
            recip(ssum, ssum)
            tt('dve', e0, e0, ssum, ALU.mult)
            tt('dve', e1, e1, ssum, ALU.mult)
            lbv = dv2[:, 93:105]
            if l == 0:
                tt('dve', lbv, e0, e0, ALU.subtract)
            else:
                tt('dve', e1, e1, e0, ALU.add)
                tt('dve', lbv, e1, e0, ALU.subtract)
            ts('dve', dv2[:, 105:117], lbv, -1.0, ALU.mult, 1.0, ALU.add)

        def compute_mod(l):
            act(scb[:], pvT[:, R_COND:R_COND + 16], AF.Silu)
            psM = psb[1]
            for s in range(36):
                slab = a16(32768 + (s % 2) * 16384, 16, 512)
                P.dma('pool', slab, w_mod[l * D:(l + 1) * D, s * 512:(s + 1) * 512].rearrange("(kc p) n -> p kc n", p=128))
                for cc in range(4):
                    col = s * 4 + cc
                    for k2 in range(KC):
                        mm(psM[:, col:col + 1], slab[:, k2, cc * 128:(cc + 1) * 128], scb[:, k2:k2 + 1],
                           start=(k2 == 0), stop=(k2 == KC - 1))
            tt('dve', modT[:], psM[:, 0:144], pvT[:, R_BMOD:R_BMOD + 144], ALU.add)
            for i in range(3):
                stt(dv[:, i, :], modT[:, (3 * i + 1) * 16:(3 * i + 2) * 16], 1.0,
                    pvT[:, R_NORMG + i * 16:R_NORMG + (i + 1) * 16], ALU.add, ALU.mult)
            ts('dve', dv[:, 3, :], modT[:, 2 * 16:3 * 16], 0.5, ALU.mult)
            cp('dve', dv[:, 4, :], modT[:, 5 * 16:6 * 16])
            ts('dve', dv[:, 5, :], modT[:, 8 * 16:9 * 16], 0.5, ALU.mult)

        RS_OFF = 0
        TMP_OFF = 4096

        def rstd_compute():
            rstd = a32(RS_OFF, T)
            for kc in range(KC):
                sq = a32(TMP_OFF + (kc % 2) * 4096, T)
                act(sq, xT[:, kc, :], AF.Square)
                for half in range(2):
                    mm(psb[2 + half][:], ones[:], sq[:, half * 512:(half + 1) * 512], start=(kc == 0), stop=(kc == KC - 1))
            for half in range(2):
                act(rstd[:, half * 512:(half + 1) * 512], psb[2 + half][:], AF.Sqrt, bias=EPS, scale=1.0 / D)
            recip(rstd, rstd)
            return rstd

        def norm_mod(scale_ap, shift_ap):
            rstd = rstd_compute()
            for kc in range(KC):
                t1 = a32(TMP_OFF + (kc % 2) * 4096, T)
                stt(t1, xT[:, kc, :], scale_ap[:, kc:kc + 1], rstd, ALU.mult, ALU.mult)
                act(hT[:, kc, :], t1, AF.Identity, bias=shift_ap[:, kc:kc + 1])

        F_W13 = 12288
        F_W2 = F_W13 + 32768
        F_ACT = F_W2 + 32768
        F_SIL = F_ACT + 16384

        def ffn(l, f, gate_ap):
            base = (l * 2 + f)
            w1d = ffn_w1[base * D:(base + 1) * D, :]
            w3d = ffn_w3[base * D:(base + 1) * D, :]
            w2d = ffn_w2[base * DFF:(base + 1) * DFF, :]
            cnt = 0
            for grp in range(11):
                s2 = grp % 2
                w2s = a16(F_W2 + s2 * 16384, 4, 2048)
                acb = a16(F_ACT + s2 * 8192, 4, 1024)
                P.dma('pool', w2s, w2d[grp * 512:(grp + 1) * 512, :].rearrange("(c p) d -> p c d", p=128))
                for pr in range(2):
                    pair = grp * 2 + pr
                    s = pair % 2
                    w1s = a16(F_W13 + s * 16384, 16, 256)
                    w3s = a16(F_W13 + s * 16384 + 8192, 16, 256)
                    P.dma('pool', w1s, w1d[:, pair * 256:(pair + 1) * 256].rearrange("(kc p) n -> p kc n", p=128))
                    P.dma('pool', w3s, w3d[:, pair * 256:(pair + 1) * 256].rearrange("(kc p) n -> p kc n", p=128))
                    for c in range(2):
                        for half in range(2):
                            q = cnt % 2
                            cnt += 1
                            g1 = psb[q]
                            g3 = psb[2 + q]
                            hs = slice(half * 512, (half + 1) * 512)
                            for kc in range(KC):
                                mm(g1[:], w1s[:, kc, c * 128:(c + 1) * 128], hT[:, kc, hs], start=(kc == 0), stop=(kc == KC - 1))
                            for kc in range(KC):
                                mm(g3[:], w3s[:, kc, c * 128:(c + 1) * 128], hT[:, kc, hs], start=(kc == 0), stop=(kc == KC - 1))
                            sil = a32(F_SIL + q * 2048, 512)
                            act(sil, g1[:], AF.Silu)
                            tt('dve', acb[:, pr * 2 + c, hs], sil, g3[:], ALU.mult)
                for dc in range(KC):
                    for half in range(2):
                        bank = psb[4 + (dc * 2 + half) % 4]
                        hs = slice(half * 512, (half + 1) * 512)
                        for c in range(4):
                            mm(bank[:], w2s[:, c, dc * 128:(dc + 1) * 128], acb[:, c, hs], start=(c == 0), stop=(c == 3))
                        stt(xT[:, dc, hs], bank[:], gate_ap[:, dc:dc + 1], xT[:, dc, hs], ALU.mult, ALU.add)

        slab_ctr = [0]

        def load_slab(l, col0, ncols):
            st = slab_ctr[0] % 2
            slab_ctr[0] += 1
            slab = a16(M_SLAB + st * 12288, 16, ncols)
            P.dma('pool', slab, w_inp[l * D:(l + 1) * D, col0:col0 + ncols].rearrange("(kc p) n -> p kc n", p=128))
            return slab

        pair_ctr = [0]

        def proj_fm(slab, c):
            pr = pair_ctr[0] % 4
            pair_ctr[0] += 1
            banks = [psb[2 * pr], psb[2 * pr + 1]]
            for half in range(2):
                for kc in range(KC):
                    mm(banks[half][:], slab[:, kc, c * 128:(c + 1) * 128], hT[:, kc, half * 512:(half + 1) * 512],
                       start=(kc == 0), stop=(kc == KC - 1))
            return banks

        wo_ctr = [0]

        def wout_group(l, chunk0, ochs):
            n = len(ochs)
            st = wo_ctr[0] % 2
            wo_ctr[0] += 1
            ws = a16(M_WOUT + st * 8192, 2, 2048)
            P.dma('pool', ws[:, 0:n, :], w_out[l * D + chunk0 * 128: l * D + (chunk0 + n) * 128, :].rearrange("(c p) d -> p c d", p=128))
            for dc in range(KC):
                for half in range(2):
                    bank = psb[4 + (dc * 2 + half) % 4]
                    hs = slice(half * 512, (half + 1) * 512)
                    for c in range(n):
                        mm(bank[:], ws[:, c, dc * 128:(dc + 1) * 128], ochs[c][:, hs], start=(c == 0), stop=(c == n - 1))
                    stt(xT[:, dc, hs], bank[:], dv[:, 4, dc:dc + 1], xT[:, dc, hs], ALU.mult, ALU.add)

        och_ctr = [0]

        def new_och():
            k = och_ctr[0] % 4
            och_ctr[0] += 1
            return a16(M_OCH + k * 2048, T)

        def mixer_a(l):
            S0 = M_SCR
            uT = [a32(S0 + g * 4096, T) for g in range(4)]
            vg = a32(S0 + 16384, 256)
            junk = a32(S0 + 25600, 256)
            vn = a16(S0 + 17920, 128)
            sm = a32(S0 + 18432, 8)
            ngb = a32(S0 + 18944, 512)
            bsb = a32(S0 + 20992, 512)
            wsT = a16(S0 + 23040, 4, 128)
            tmpo = a32(S0 + 24064, 128)
            ochs = [new_och() for _ in range(4)]
            P.dma('sp', ngb, bass.AP(ngb_d.tensor, l * 512, [[0, 128], [1, 512]]))
            P.dma('sp', bsb, bass.AP(bsb_d.tensor, l * 512, [[0, 128], [1, 512]]))
            P.dma('pool', wsT, wsT_d[l * 512:(l + 1) * 512, :].rearrange("(g q) p -> q g p", q=128))
            for s in range(2):
                slab = load_slab(l, s * 256, 256)
                for c in range(2):
                    banks = proj_fm(slab, c)
                    for half in range(2):
                        act(uT[s * 2 + c][:, half * 512:(half + 1) * 512], banks[half][:], AF.Gelu)
            k = 0
            for s in range(2):
                slab = load_slab(l, 512 + s * 256, 256)
                for i in range(NT):
                    tsl = slice(i * 128, (i + 1) * 128)
                    bank = psb[i % 2]
                    for kc in range(KC):
                        mm(bank[:, 0:256], hT[:, kc, tsl], slab[:, kc, :], start=(kc == 0), stop=(kc == KC - 1))
                    act(vg, bank[:, 0:256], AF.Gelu)
                    act(junk, vg, AF.Square)
                    j3 = junk.rearrange("p (a b) -> p a b", a=2)
                    P.op('dve', lambda e, j3=j3: e.tensor_reduce(out=sm[:, 0:2], in_=j3, axis=mybir.AxisListType.X, op=ALU.add),
                         [junk], [sm[:, 0:2]])
                    act(sm[:, 2:4], sm[:, 0:2], AF.Sqrt, bias=EPS, scale=1.0 / 128)
                    recip(sm[:, 4:6], sm[:, 2:4])
                    for c in range(2):
                        g = s * 2 + c
                        vgc = vg[:, c * 128:(c + 1) * 128]
                        stt(vn, vgc, sm[:, 4 + c:5 + c], ngb[:, g * 128:(g + 1) * 128], ALU.mult, ALU.mult)
                        pb = psb[2 + (k % 2)]
                        k += 1
                        mm(pb[:, 0:128], vn, wsT[:, g, :])
                        tt('dve', tmpo, pb[:, 0:128], bsb[:, g * 128:(g + 1) * 128], ALU.add)
                        tt('dve', ochs[g][:, tsl], tmpo, uT[g][:, tsl], ALU.mult)
            if dbg and l == 0:
                for g in range(4):
                    P.dma('pool', dbg_out[:, g * T:(g + 1) * T], ochs[g])
            wout_group(l, 0, ochs[0:2])
            wout_group(l, 2, ochs[2:4])

        def mixer_c(l):
            S0 = M_SCR
            qs = a32(S0, T)
            gs = a16(S0 + 4096, T)
            vtok = a16(S0 + 6144, 8, 128)
            oacc = a32(S0 + 8192, T)
            bA = a32(S0 + 12288, T)
            bB = a32(S0 + 16384, T)
            bC = a32(S0 + 20480, T)
            bD = a32(S0 + 24576, T)
            sgb = a32(S0 + 28672, T)
            vmask = a16(S0 + 32768, 8, 128)
            attm = a16(S0 + 34816, 128)
            ktok = a16(S0 + 35072, 128)
            cdec = a32(S0 + 35328, 64)
            sall = [a32(S0 + 35584 + k * 4096, 8, 128) for k in range(2)]
            tmpS = a32(S0 + 43776, 128)
            sqo = a32(S0 + 44288, T)
            assert S0 + 48384 <= ARENA_B
            ochs = []
            for h in range(6):
                base_c = 3712 + h * 640
                slab1 = load_slab(l, base_c, 384)
                bq = proj_fm(slab1, 0)
                for half in range(2):
                    act(qs[:, half * 512:(half + 1) * 512], bq[half][:], AF.Silu)
                bf = proj_fm(slab1, 1)
                for half in range(2):
                    act(bA[:, half * 512:(half + 1) * 512], bf[half][:], AF.Sigmoid)
                bfb = proj_fm(slab1, 2)
                for half in range(2):
                    act(sgb[:, half * 512:(half + 1) * 512], bfb[half][:], AF.Sigmoid)
                slab2 = load_slab(l, base_c + 384, 256)
                bg = proj_fm(slab2, 0)
                for half in range(2):
                    act(gs[:, half * 512:(half + 1) * 512], bg[half][:], AF.Silu)
                for i4 in range(2):
                    bank = psb[i4]
                    for j in range(4):
                        i = i4 * 4 + j
                        for kc in range(KC):
                            mm(bank[:, j * 128:(j + 1) * 128], hT[:, kc, i * 128:(i + 1) * 128], slab2[:, kc, 128:256],
                               start=(kc == 0), stop=(kc == KC - 1))
                    cp('act', vtok[:, i4 * 4:(i4 + 1) * 4, :], bank[:].rearrange("p (a b) -> p a b", a=4))
                for d in range(2):
                    lb = dv2[:, 93 + d * 6 + h: 94 + d * 6 + h]
                    omlb = dv2[:, 105 + d * 6 + h: 106 + d * 6 + h]
                    src = bA if d == 0 else sgb
                    ts('dve', bA, src, omlb, ALU.mult, lb, ALU.add)
                    act(bB, bA, AF.Ln)
                    ts('dve', bA, bA, -1.0, ALU.mult, 1.0, ALU.add)
                    rm = cst[:, C_RM:C_RM + 128]
                    for i in range(NT):
                        tsl = slice(i * 128, (i + 1) * 128)
                        if d == 0:
                            scan(bC[:, tsl], rm, bB[:, tsl], [rm, bB[:, tsl]], [bC[:, tsl]])
                        else:
                            scan(rev(bC[:, tsl]), rm, rev(bB[:, tsl]), [rm, bB[:, tsl]], [bC[:, tsl]])
                    act(bB, bC, AF.Exp)
                    tt('dve', bB, bB, qs, ALU.mult)
                    act(bD, bC, AF.Exp, scale=-1.0)
                    tt('dve', bD, bD, bA, ALU.mult)
                    eoff = 15 if d == 0 else 0
                    bend = bass.AP(bC.tensor, bC.offset + eoff, [list(bC.ap[0]), [16, 64], [0, 16]])
                    bend1 = bass.AP(bC.tensor, bC.offset + eoff, [list(bC.ap[0]), [16, 64]])
                    act(cdec, bend1, AF.Exp)
                    bC3 = bC.rearrange("p (a b) -> p a b", b=16)
                    sq3 = sqo.rearrange("p (a b) -> p a b", b=16)
                    tt('dve', sq3, bend, bC3, ALU.subtract)
                    act(sqo, sqo, AF.Exp)
                    tt('dve', bC, sqo, bA, ALU.mult)
                    hm = cst[:, C_HMF:C_HMF + 128] if d == 0 else cst[:, C_HMB:C_HMB + 128]
                    bm3 = bass.AP(cst[:].tensor, C_BM, [list(cst[:].ap[0]), [1, 8], [0, 128]])
                    order = list(range(NT)) if d == 0 else list(range(NT - 1, -1, -1))
                    srow = ((l * 2 + d) * 6 + h) * 128
                    for n_i, i in enumerate(order):
                        tsl = slice(i * 128, (i + 1) * 128)
                        tb = n_i % 2
                        if n_i == 0:
                            P.dma('sp', sall[tb][:, 0, :], shg0[srow:srow + 128, :])
                        pa = psb[0 + 4 * tb]
                        pk = psb[5 + tb]
                        mm(pa[:, 0:128], bD[:, tsl], bB[:, tsl])
                        tt('dve', attm, pa[:, 0:128], hm, ALU.mult)
                        tr(pk[:, 0:128], bC[:, tsl], ident[:])
                        cp('act', ktok, pk[:, 0:128])
                        v3 = bass.AP(vtok.tensor, vtok[:, i, :].offset, [list(vtok.ap[0]), [0, 8], [1, 128]])
                        tt('dve', vmask, v3, bm3, ALU.mult)
                        kv = [psb[1], psb[2]]
                        for q in range(2):
                            mm(kv[q][:], ktok, vmask[:, q * 4:(q + 1) * 4, :])
                        po = psb[3 + 4 * tb]
                        mm(po[:, 0:128], vtok[:, i, :], attm, start=True, stop=False)
                        corder = list(range(8)) if d == 0 else list(range(7, -1, -1))
                        for s_i, c in enumerate(corder):
                            cur = sall[tb][:, s_i, :]
                            mm(po[:, c * 16:(c + 1) * 16], cur, bB[:, i * 128 + c * 16: i * 128 + (c + 1) * 16],
                               start=False, stop=(s_i == 7))
                            kvc = kv[c // 4][:, (c % 4) * 128:(c % 4 + 1) * 128]
                            seg_end = (s_i == 7 and n_i % 2 == 1)
                            if s_i < 7:
                                nxt = sall[tb][:, s_i + 1, :]
                            elif seg_end:
                                nxt = tmpS
                            else:
                                nxt = sall[1 - tb][:, 0, :]
                            stt(nxt, cur, cdec[:, i * 8 + c: i * 8 + c + 1], kvc, ALU.mult, ALU.add)
                        if n_i % 2 == 1:
                            seg = i // 2
                            orow = (((seg * depth + l) * 2 + d) * 6 + h) * 128
                            P.dma('sp', shg_out[orow:orow + 128, :], tmpS)
                            if n_i < NT - 1:
                                ts('dve', sall[1 - tb][:, 0, :], tmpS, chain, ALU.mult)
                        if d == 0:
                            cp('act', oacc[:, tsl], po[:, 0:128])
                        else:
                            tt('dve', oacc[:, tsl], oacc[:, tsl], po[:, 0:128], ALU.add)
                act(sqo, oacc, AF.Square)
                for half in range(2):
                    mm(psb[4 + half][:], ones[:], sqo[:, half * 512:(half + 1) * 512])
                for half in range(2):
                    act(sqo[:, half * 512:(half + 1) * 512], psb[4 + half][:], AF.Sqrt, bias=EPS, scale=1.0 / 128)
                recip(sqo, sqo)
                stt(oacc, oacc, pvT[:, R_GN:R_GN + 1], sqo, ALU.mult, ALU.mult)
                och = new_och()
                tt('dve', och, oacc, gs, ALU.mult)
                ochs.append(och)
                if dbg and l == 0:
                    P.dma('pool', dbg_out[:, (10 + h) * T:(11 + h) * T], och)
                if h % 2 == 1:
                    wout_group(l, 10 + h - 1, ochs[-2:])

        def mixer_b(l):
            S0 = M_WOUT + 8192
            twd = a16(S0, T)
            adT = a16(S0 + 2048, T)
            sgd = a16(S0 + 4096, T)
            rF = a32(S0 + 6144, T)
            kF = a32(S0 + 10240, T)
            vF = a32(S0 + 14336, T)
            kkF = a32(S0 + 18432, T)
            aF = a32(S0 + 22528, T)
            sgF = a32(S0 + 26624, T)
            yacc = a32(S0 + 30720, T)
            rkd = a32(S0 + 34816, T)
            Q0 = S0 + 38912
            zpad = a32(Q0, 1026)
            zsum = a32(Q0 + 4104, T)
            zm = a32(Q0 + 8200, T)
            cw = a32(Q0, 128)
            w1_ = a32(Q0 + 512, 128)
            w2_ = a32(Q0 + 1024, 128)
            w3_ = a32(Q0 + 1536, 128)
            kd_ = a32(Q0 + 2048, 128)
            b_ = cw
            AR = a32(Q0 + 2560, 2, 128)
            BT = a32(Q0 + 3584, 128)
            KT = a32(Q0 + 4096, 128)
            TOK = a32(Q0 + 4608, 4, 128)
            G1m = [a32(Q0 + 6656 + hh * 1024, 256) for hh in range(2)]
            G2m = [a32(Q0 + 8704 + hh * 1024, 256) for hh in range(2)]
            Xb = [[a32(Q0 + 10752 + hh * 1024 + k * 512, 128) for k in range(2)] for hh in range(2)]
            PQ = [[a32(Q0 + 12800 + hh * 2048 + k * 1024, 256) for k in range(2)] for hh in range(2)]
            MT = a16(Q0 + 16896, 64)
            RT = a16(Q0 + 17152, 128)
            Sst = [a32(Q0 + 17664 + k * 256, 64) for k in range(2)]
            Scb = a16(Q0 + 18176, 64)
            BTb = a16(Q0 + 1024, 128)
            KTb = a16(Q0 + 1280, 128)
            ARb = a16(Q0 + 2048, 2, 128)
            tmp_ = w3_
            assert Q0 + 18304 <= ARENA_B, Q0 + 18304
            wl = a16(Q0 + 12296, 768)
            mu_o = dv2[:, 0:21]
            mu_h = dv2[:, 21:42]
            mu_n = dv2[:, 42:63]

            def shift_mix(banks, mi, out, func=None):
                P.op('pool', lambda e: e.memset(zpad[:, 0:1], 0.0), [], [zpad[:, 0:1]])
                P.op('pool', lambda e: e.memset(zpad[:, 1025:1026], 0.0), [], [zpad[:, 1025:1026]])
                for half in range(2):
                    cp('act', zpad[:, 1 + half * 512: 1 + (half + 1) * 512], banks[half][:])
                tt('dve', zsum, zpad[:, 0:1024], zpad[:, 2:1026], ALU.add)
                act(zm, zpad[:, 1:1025], AF.Identity, scale=mu_o[:, mi:mi + 1])
                stt(zm, zsum, mu_h[:, mi:mi + 1], zm, ALU.mult, ALU.add)
                zc = zpad[:, 1:1025]
                for (dst0, src0) in ((256, 255), (255, 256)):
                    dsts = bass.AP(zm.tensor, zm.offset + dst0, [list(zm.ap[0]), [256, 3]])
                    srcs = bass.AP(zc.tensor, zc.offset + src0, [list(zc.ap[0]), [256, 3]])
                    stt(dsts, srcs, mu_n[:, mi:mi + 1], dsts, ALU.mult, ALU.add)
                if func is None:
                    cp('dve', out, zm)
                else:
                    act(out, zm, func)

            slab = load_slab(l, 1024, 384)
            shift_mix(proj_fm(slab, 0), 0, twd, AF.Tanh)
            shift_mix(proj_fm(slab, 1), 1, adT)
            shift_mix(proj_fm(slab, 2), 2, sgd, AF.Sigmoid)
            ochs = []
            for hp in range(rwk[0]):
                hcols = slice(hp * 128, (hp + 1) * 128)
                slab = load_slab(l, 1408 + hp * 384, 384)
                shift_mix(proj_fm(slab, 0), 3 + hp * 3 + 0, rF)
                shift_mix(proj_fm(slab, 1), 3 + hp * 3 + 1, kF)
                shift_mix(proj_fm(slab, 2), 3 + hp * 3 + 2, vF)
                ts('dve', kkF, kF, pvT[:, R_KK + hp:R_KK + hp + 1], ALU.mult)
                tt('dve', zsum, kkF, kkF, ALU.mult)
                for half in range(2):
                    mm(psb[half][:], cst[:, C_BD64:C_BD64 + 128], zsum[:, half * 512:(half + 1) * 512])
                for half in range(2):
                    act(zsum[:, half * 512:(half + 1) * 512], psb[half][:], AF.Sqrt)
                ts('dve', zsum, zsum, 1e-12, ALU.max)
                recip(zsum, zsum)
                tt('dve', kkF, kkF, zsum, ALU.mult)
                for d in range(rwk[1]):
                    ds = slice(64 * d, 64 * d + 64)
                    P.dma('pool', wl[ds, :], rw2_d[l * 128 + 64 * d: l * 128 + 64 * d + 64, :])
                    bk = [psb[2], psb[3]]
                    for half in range(2):
                        mm(bk[half][:], wl[ds, hcols], twd[ds, half * 512:(half + 1) * 512])
                    for half in range(2):
                        act(sgF[:, half * 512:(half + 1) * 512], bk[half][:], AF.Sigmoid,
                            bias=pvT[:, R_W0 + d * 6 + hp:R_W0 + d * 6 + hp + 1])
                    P.dma('pool', wl[ds, :], ra2_d[l * 128 + 64 * d: l * 128 + 64 * d + 64, :])
                    bk = [psb[4], psb[5]]
                    for half in range(2):
                        mm(bk[half][:], wl[ds, hcols], adT[ds, half * 512:(half + 1) * 512])
                    for half in range(2):
                        act(aF[:, half * 512:(half + 1) * 512], bk[half][:], AF.Sigmoid,
                            bias=pvT[:, R_A0 + d * 6 + hp:R_A0 + d * 6 + hp + 1])
                    order = list(range(NT)) if d == 0 else list(range(NT - 1, -1, -1))
                    MT2 = cst[:, C_MT2F:C_MT2F + 256] if d == 0 else cst[:, C_MT2B:C_MT2B + 256]
                    MS = cst[:, C_SL:C_SL + 128] if d == 0 else cst[:, C_SU:C_SU + 128]
                    srow = ((l * 2 + d) * 12 + 2 * hp) * 64
                    for n_i, i in enumerate(order[:rwk[2]]):
                        tsl = slice(i * 128, (i + 1) * 128)
                        Sc = Sst[n_i % 2]
                        Sn = Sst[1 - n_i % 2]
                        if n_i == 0:
                            P.dma('sp', Sc, srw0[srow:srow + 128, :])
                        if d == 0:
                            scan(cw, ones[:], sgF[:, tsl], [ones[:], sgF[:, tsl]], [cw])
                        else:
                            scan(rev(cw), ones[:], rev(sgF[:, tsl]), [ones[:], sgF[:, tsl]], [cw])
                        act(w1_, cw, AF.Exp, scale=-DECAY_C)
                        act(w2_, cw, AF.Exp, scale=DECAY_C)
                        tt('dve', w3_, cw, sgF[:, tsl], ALU.subtract)
                        act(w3_, w3_, AF.Exp, scale=-DECAY_C)
                        ts('dve', kd_, aF[:, tsl], pvT[:, R_KA + hp:R_KA + hp + 1], ALU.mult, dv2[:, 63 + hp:64 + hp], ALU.add)
                        tt('dve', kd_, kd_, kF[:, tsl], ALU.mult)
                        tt('dve', b_, kkF[:, tsl], aF[:, tsl], ALU.mult)
                        stt(AR[:, 0, :], kkF[:, tsl], -1.0, w3_, ALU.mult, ALU.mult)
                        stt(tmp_, rF[:, tsl], pvT[:, R_RK + hp:R_RK + hp + 1], kd_, ALU.mult, ALU.mult)
                        if d == 0:
                            cp('dve', rkd[:, tsl], tmp_)
                        else:
                            tt('dve', rkd[:, tsl], rkd[:, tsl], tmp_, ALU.add)
                        tt('dve', AR[:, 1, :], rF[:, tsl], w1_, ALU.mult)
                        tt('dve', BT, b_, w2_, ALU.mult)
                        tt('dve', KT, kd_, w2_, ALU.mult)
                        if rwk[3] < 1:
                            continue
                        pt = psb[0]
                        _rwx = ''
                        if 'a' not in _rwx:
                            mm(pt[:, 0:128], AR[:, 0, :], ident[:])
                        if 'b' not in _rwx:
                            mm(pt[:, 128:256], BT, ident[:])
                        if 'k' not in _rwx:
                            mm(pt[:, 256:384], KT, ident[:])
                        if 'v' not in _rwx:
                            mm(pt[:, 384:512], vF[:, tsl], ident[:])
                        if 'c' not in _rwx:
                            cp('act', TOK, pt[:].rearrange("p (a b) -> p a b", a=4))
                        if 'd' in _rwx:
                            P.dma('sp', dbg_out[:, 0:256], AR.rearrange("p a b -> p (a b)"))
                            P.dma('sp', dbg_out[:, 256:384], BT)
                            P.dma('sp', dbg_out[:, 384:512], KT)
                            P.dma('sp', dbg_out[:, 512:640], cw)
                            P.dma('sp', dbg_out[:, 640:768], w2_)
                            P.dma('sp', dbg_out[:, 1024:2048], sgF)
                            P.dma('sp', dbg_out[:, 2048:3072], kkF)
                            P.dma('sp', dbg_out[:, 3072:4096], aF)
                        if rwk[3] < 2:
                            continue
                        if 'p' not in _rwx:
                            cp('pool', ARb, AR)
                            cp('pool', BTb, BT)
                            cp('pool', KTb, KT)
                            cp('pool', Scb, Sc)
                        pG = [psb[1], psb[2]]
                        pB = [psb[3], psb[4]]
                        hsl = [slice(0, 64), slice(64, 128)]
                        for hh in range(2):
                            hs = hsl[hh]
                            if 'g' not in _rwx:
                                mm(pG[hh][:, 0:256], BTb[hs, :], ARb[hs, :, :])
                                mm(pG[hh][:, 256:512], KTb[hs, :], ARb[hs, :, :])
                            if '3' not in _rwx:
                                mm(pB[hh][:, 0:128], ARb[hs, 0, :], BTb[hs, :])
                            if 'm' not in _rwx:
                                if '6' not in _rwx:
                                    tt('dve', G2m[hh], pG[hh][:, 256:512], MT2, ALU.mult)
                                if '5' not in _rwx:
                                    tt('dve', G1m[hh], pG[hh][:, 0:256], MT2, ALU.mult)
                                if '7' not in _rwx:
                                    tt('dve', PQ[hh][0][:, 128:256], pB[hh][:, 0:128], MS, ALU.mult)
                                if '8' not in _rwx:
                                    cp('act', PQ[hh][0][:, 0:128], G1m[hh][:, 0:128])
                            if 'q' not in _rwx:
                                mm(psb[5 + hh][:, 128:192], G2m[hh][:, 0:128], TOK[:, 3, hs])
                                cp('act', Xb[hh][0][:, 0:64], TOK[:, 0, hs])
                                cp('act', Xb[hh][0][:, 64:128], psb[5 + hh][:, 128:192])
                        if rwk[3] < 3:
                            continue
                        pC = [psb[5], psb[6]]
                        for lvl in range(7):
                            for hh in range(2):
                                Pm = PQ[hh][lvl % 2][:, 0:128]
                                Qm = PQ[hh][lvl % 2][:, 128:256]
                                Xc = Xb[hh][lvl % 2]
                                Xn = Xb[hh][1 - lvl % 2]
                                mm(pC[hh][:, 0:128], Pm, Xc)
                                tt('dve', Xn, Xc, pC[hh][:, 0:128], ALU.add)
                                if lvl < 6:
                                    mm(pG[hh][:, 0:128], Qm, Pm)
                                    mm(pG[hh][:, 128:256], Pm, Qm)
                                    cp('act', PQ[hh][1 - lvl % 2], pG[hh][:, 0:256])
                        if rwk[3] < 4:
                            continue
                        pEs = [psb[3], psb[0]]
                        pRs = [psb[1], psb[2]]
                        pSs = [psb[7], psb[5]]
                        pYs = [psb[4], psb[6]]
                        wend = w1_[:, 127:128] if d == 0 else w1_[:, 0:1]
                        for hh in range(2):
                            hs = hsl[hh]
                            Xf = Xb[hh][1]
                            pE, pR, pS, pY = pEs[hh], pRs[hh], pSs[hh], pYs[hh]
                            mm(pE[hs, 0:64], Xf[:, 0:64], TOK[:, 1, hs])
                            tt('dve', MT[hs, :], pE[hs, 0:64], ident[hs, hs], ALU.add)
                            mm(pR[hs, 0:128], Xf[:, 0:64], G1m[hh][:, 128:256])
                            tt('dve', RT[hs, :], pR[hs, 0:128], AR[hs, 1, :], ALU.add)
                            mm(pS[hs, 0:64], TOK[:, 1, hs], Xf[:, 64:128], start=True, stop=False)
                            mm(pS[hs, 0:64], TOK[:, 2, hs], TOK[:, 3, hs], start=False, stop=False)
                            mm(pS[hs, 0:64], MT[hs, :], Scb[hs, :], start=False, stop=True)
                            mm(pY[hs, 0:128], Scb[hs, :], RT[hs, :], start=True, stop=False)
                            mm(pY[hs, 0:128], Xf[:, 64:128], G1m[hh][:, 128:256], start=False, stop=False)
                            mm(pY[hs, 0:128], TOK[:, 3, hs], G2m[hh][:, 128:256], start=False, stop=True)
                        for hh in range(2):
                            hs = hsl[hh]
                            pS, pY = pSs[hh], pYs[hh]
                            if d == 0:
                                cp('act', yacc[hs, tsl], pY[hs, 0:128])
                            else:
                                tt('dve', yacc[hs, tsl], yacc[hs, tsl], pY[hs, 0:128], ALU.add)
                            if n_i % 2 == 1:
                                ts('dve', tmp_[hs, 0:64], pS[hs, 0:64], wend[hs, :], ALU.mult)
                            else:
                                ts('dve', Sn[hs, :], pS[hs, 0:64], wend[hs, :], ALU.mult)
                        if n_i % 2 == 1:
                            san(tmp_[:, 0:64])
                            seg = i // 2
                            orow = (((seg * depth + l) * 2 + d) * 12 + 2 * hp) * 64
                            P.dma('sp', srw_out[orow:orow + 128, :], tmp_[:, 0:64])
                            if n_i < NT - 1:
                                ts('dve', Sn, tmp_[:, 0:64], chain, ALU.mult)
                bd = cst[:, C_BD64:C_BD64 + 128]
                for half in range(2):
                    mm(psb[half][:], bd, yacc[:, half * 512:(half + 1) * 512])
                for half in range(2):
                    hsz = slice(half * 512, (half + 1) * 512)
                    stt(yacc[:, hsz], psb[half][:], -1.0 / 64, yacc[:, hsz], ALU.mult, ALU.add)
                tt('dve', aF, yacc, yacc, ALU.mult)
                for half in range(2):
                    mm(psb[2 + half][:], bd, aF[:, half * 512:(half + 1) * 512])
                for half in range(2):
                    act(aF[:, half * 512:(half + 1) * 512], psb[2 + half][:], AF.Sqrt, bias=LNX_EPS, scale=1.0 / 64)
                recip(aF, aF)
                stt(yacc, yacc, pvT[:, R_LNG + hp:R_LNG + hp + 1], aF, ALU.mult, ALU.mult)
                for half in range(2):
                    mm(psb[4 + half][:], bd, rkd[:, half * 512:(half + 1) * 512])
                for half in range(2):
                    hsz = slice(half * 512, (half + 1) * 512)
                    tt('dve', aF[:, hsz], psb[4 + half][:], vF[:, hsz], ALU.mult)
                stt(yacc, yacc, pvT[:, R_LNB + hp:R_LNB + hp + 1], aF, ALU.add, ALU.add)
                P.dma('pool', wl[:, :], rg2_d[l * 128:(l + 1) * 128, :])
                och = new_och()
                for half in range(2):
                    mm(psb[6 + half][:], wl[:, hcols], sgd[:, half * 512:(half + 1) * 512])
                for half in range(2):
                    hsz = slice(half * 512, (half + 1) * 512)
                    tt('dve', och[:, hsz], yacc[:, hsz], psb[6 + half][:], ALU.mult)
                ochs.append(och)
                if dbg and l == 0:
                    san(och)
                    P.dma('pool', dbg_out[:, (4 + hp) * T:(5 + hp) * T], och)
                    san(yacc)
                    P.dma('sp', dbg_out[:, 0:T], yacc)
                if hp % 2 == 1:
                    wo_ctr[0] = 0
                    wout_group(l, 4 + hp - 1, ochs[-2:])

        for l in range(depth):
            load_pv(l)
            if 'mod' in parts:
                compute_mod(l)
            else:
                P.op('pool', lambda e: e.memset(modT[:], 0.0), [], [modT[:]])
                for i in range(3):
                    cp('dve', dv[:, i, :], pvT[:, R_NORMG + i * 16:R_NORMG + (i + 1) * 16])
                P.op('pool', lambda e: e.memset(dv[:, 3:6, :], 1.0), [], [dv[:, 3:6, :]])
            if 'ffn1' in parts:
                norm_mod(dv[:, 0, :], modT[:, 0:16])
                ffn(l, 0, dv[:, 3, :])
            if any(m in parts for m in 'abc'):
                norm_mod(dv[:, 1, :], modT[:, 3 * 16:4 * 16])
                if 'a' in parts:
                    mixer_a(l)
                if 'b' in parts:
                    mixer_b(l)
                if 'c' in parts:
                    mixer_c(l)
            if 'ffn2' in parts:
                norm_mod(dv[:, 2, :], modT[:, 6 * 16:7 * 16])
                ffn(l, 1, dv[:, 5, :])

        rstd = rstd_compute()
        fg = pvT[:, R_FG:R_FG + 16]
        for i in range(NT):
            ys = a32(16384 + (i % 2) * 8192, 2048)
            tsl = slice(i * 128, (i + 1) * 128)
            for q in range(4):
                bank = psb[4 + (i * 4 + q) % 4]
                for j in range(4):
                    kc = q * 4 + j
                    t1 = a32(40960 + (kc % 4) * 512, 128)
                    stt(t1, xT[:, kc, tsl], fg[:, kc:kc + 1], rstd[:, tsl], ALU.mult, ALU.mult)
                    tr(bank[:, j * 128:(j + 1) * 128], t1, ident[:])
                cp('act', ys[:, q * 512:(q + 1) * 512], bank[:])
            san(ys)
            P.dma('sp', y_out[tsl, :], ys)
        P.final_wait_all('sp')
        P.build()
    return nc


def _perm_cols():
    A = list(range(1024))
    B0 = 1024
    C0 = 3712
    lora = list(range(B0 + 2304, B0 + 2688))
    b = []
    for hp in range(6):
        for part in range(3):
            b += list(range(B0 + part * 768 + hp * 128, B0 + part * 768 + (hp + 1) * 128))
    c = []
    for h in range(6):
        for part in (0, 1, 2, 4, 3):
            c += list(range(C0 + part * 768 + h * 128, C0 + part * 768 + (h + 1) * 128))
    return np.array(A + lora + b + c)


def _consts():
    j = np.arange(128)[:, None]
    t = np.arange(128)[None, :]
    su = (t > j).astype(np.float32)
    iu = (t >= j).astype(np.float32)
    sl = (t < j).astype(np.float32)
    il = (t <= j).astype(np.float32)
    same = (j // 16 == t // 16)
    hmf = (same & (j <= t)).astype(np.float32)
    hmb = (same & (j >= t)).astype(np.float32)
    bd = (j // 64 == t // 64).astype(np.float32)
    bm = (j // 16 == np.arange(8)[None, :]).astype(np.float32)
    rm = np.broadcast_to((np.arange(128) % 16 != 0).astype(np.float32)[None, :], (128, 128))
    return np.ascontiguousarray(np.concatenate([su, iu, sl, il, sl, su, hmf, hmb, bd, bm, rm], axis=1))


def _pack_pvec(I, cond, depth):
    rows = np.zeros((depth, NPV, 128), np.float32)
    mu_order = [18, 19, 20] + [part * 6 + hp for hp in range(6) for part in range(3)]
    for l in range(depth):
        r = rows[l]
        r[R_NORMG:R_NORMG + 48] = I['norm_g'][l].reshape(48, 128)
        r[R_BMOD:R_BMOD + 144] = I['b_mod'][l].reshape(144, 128)
        r[R_MU:R_MU + 21] = I['rwkv_mu'][l].reshape(21, 128)[mu_order]
        r[R_W0:R_W0 + 12] = I['rwkv_w0'][l].reshape(12, 128)
        r[R_A0:R_A0 + 12] = I['rwkv_a0'][l].reshape(12, 128)
        r[R_KK:R_KK + 6] = I['rwkv_kk'][l].reshape(6, 128)
        r[R_KA:R_KA + 6] = I['rwkv_ka'][l].reshape(6, 128)
        r[R_RK:R_RK + 6] = I['rwkv_rk'][l].reshape(6, 128)
        r[R_LNG:R_LNG + 6] = I['rwkv_lnx_g'][l].reshape(6, 128)
        r[R_LNB:R_LNB + 6] = I['rwkv_lnx_b'][l].reshape(6, 128)
        r[R_LB:R_LB + 24] = I['hgrn_lb'].reshape(24, 128)
        r[R_GN] = I['hgrn_gn'][l]
        r[R_FG:R_FG + 16] = I['final_g'].reshape(16, 128)
        r[R_COND:R_COND + 16] = cond.reshape(16, 128)
    return rows.reshape(depth * NPV, 128)


def _shared_inputs(I, depth):
    perm = _perm_cols()
    return dict(
        consts=_consts(),
        w_mod=I['w_mod'][:depth].reshape(depth * D, 9 * D),
        ffn_w1=I['ffn_w1'][:depth].reshape(depth * 2 * D, DFF),
        ffn_w3=I['ffn_w3'][:depth].reshape(depth * 2 * D, DFF),
        ffn_w2=I['ffn_w2'][:depth].reshape(depth * 2 * DFF, D),
        w_inp=np.ascontiguousarray(I['w_in'][:depth][:, :, perm]).reshape(depth * D, IN_COLS),
        w_out=I['w_out'][:depth].reshape(depth * D, D),
        wsT=np.ascontiguousarray(np.swapaxes(I['mlp_ws'][:depth], -1, -2)).reshape(depth * 512, 128),
        ngb=I['mlp_norm_g'][:depth].reshape(depth, 512),
        bsb=I['mlp_bs'][:depth].reshape(depth, 512),
        rw2=I['rwkv_w2'][:depth].reshape(depth * 128, 768),
        ra2=I['rwkv_a2'][:depth].reshape(depth * 128, 768),
        rg2=I['rwkv_g2'][:depth].reshape(depth * 128, 768),
    )


def _core_inputs(I, core, depth, shared):
    m = dict(shared)
    if core < 4:
        m['x_in'] = np.ascontiguousarray(I['x_prompt'][4 * core:4 * core + 4].reshape(T, D))
        cond = I['c_ctx']
        m['flags'] = np.tile(np.array([[0.0, 1.0]], np.float32), (128, 1))
        m['srw0'] = np.zeros((depth * 2 * 12 * 64, 64), np.float32)
        m['shg0'] = np.zeros((depth * 2 * 6 * 128, 128), np.float32)
    else:
        b = core - 4
        m['x_in'] = np.ascontiguousarray(I['x_sample'][b])
        cond = I['c'][b]
        m['flags'] = np.tile(np.array([[1.0, 0.0]], np.float32), (128, 1))
        m['srw0'] = np.ascontiguousarray(np.swapaxes(I['state_rwkv'][b, :depth], -1, -2)).reshape(depth * 2 * 12 * 64, 64)
        m['shg0'] = np.ascontiguousarray(I['state_hgrn'][b, :depth]).reshape(depth * 2 * 6 * 128, 128)
    m['pvec'] = _pack_pvec(I, cond, depth)
    return m


def _unpack_states(R, depth):
    rw = R['srw_out'].reshape(4, depth, 2, 12, 64, 64)
    rw = np.ascontiguousarray(np.swapaxes(rw, -1, -2))
    hg = R['shg_out'].reshape(4, depth, 2, 6, 128, 128)
    return rw, hg


_NC_CACHE = {}


def kernel(x_prompt, x_sample, state_rwkv, state_hgrn, c, c_ctx, norm_g, w_mod, b_mod,
           ffn_w1, ffn_w3, ffn_w2, w_in, w_out, mlp_norm_g, mlp_ws, mlp_bs, rwkv_mu,
           rwkv_w0, rwkv_w2, rwkv_a0, rwkv_a2, rwkv_g2, rwkv_kk, rwkv_ka, rwkv_rk,
           rwkv_lnx_g, rwkv_lnx_b, hgrn_lb, hgrn_gn, final_g):
    I = {k: np.asarray(v, np.float32) for k, v in dict(
        x_prompt=x_prompt, x_sample=x_sample, state_rwkv=state_rwkv, state_hgrn=state_hgrn, c=c, c_ctx=c_ctx,
        norm_g=norm_g, w_mod=w_mod, b_mod=b_mod, ffn_w1=ffn_w1, ffn_w3=ffn_w3, ffn_w2=ffn_w2, w_in=w_in,
        w_out=w_out, mlp_norm_g=mlp_norm_g, mlp_ws=mlp_ws, mlp_bs=mlp_bs, rwkv_mu=rwkv_mu, rwkv_w0=rwkv_w0,
        rwkv_w2=rwkv_w2, rwkv_a0=rwkv_a0, rwkv_a2=rwkv_a2, rwkv_g2=rwkv_g2, rwkv_kk=rwkv_kk, rwkv_ka=rwkv_ka,
        rwkv_rk=rwkv_rk, rwkv_lnx_g=rwkv_lnx_g, rwkv_lnx_b=rwkv_lnx_b, hgrn_lb=hgrn_lb, hgrn_gn=hgrn_gn,
        final_g=final_g).items()}
    if 'nc' not in _NC_CACHE:
        _NC_CACHE['nc'] = build_program()
    nc = _NC_CACHE['nc']
    shared = _shared_inputs(I, DEPTH)
    in_maps = [_core_inputs(I, core, DEPTH, shared) for core in range(6)]
    res = run_bass_kernel_spmd(nc, in_maps, core_ids=list(range(6)))
    R = res.results
    y_prompt = np.stack([R[cc]['y_out'].reshape(4, 256, D) for cc in range(4)]).reshape(16, 256, D)
    y_sample = np.stack([R[4]['y_out'], R[5]['y_out']])
    rws, hgs = [], []
    for cc in range(4):
        rw, hg = _unpack_states(R[cc], DEPTH)
        rws.append(rw)
        hgs.append(hg)
    new_rw = np.concatenate(rws, axis=0)
    new_hg = np.concatenate(hgs, axis=0)
    return (np.ascontiguousarray(y_prompt, np.float32), np.ascontiguousarray(y_sample, np.float32),
            np.ascontiguousarray(new_rw, np.float32), np.ascontiguousarray(new_hg, np.float32))
```
